# Optimizing a Trainium2 kernel written in Bass

```python
import math
import jax, jax.numpy as jnp
from jax import lax
import numpy as np

D_MODEL = 2048
BATCH = 4
SEQ = 4096
DEPTH = 1

D_MIX = D_MODEL
POOL_WIDTH = D_MIX // 2
SSM_WIDTH = D_MIX - POOL_WIDTH
POOL_WINDOWS = (2, 4, 8, 16)
N_POOL_GROUPS = len(POOL_WINDOWS)
POOL_GROUP = POOL_WIDTH // N_POOL_GROUPS
SSM_GROUP = 16
N_SSM_GROUPS = SSM_WIDTH // SSM_GROUP
SSM_STATE = 64
PLE_DIM = 256
EPS = 1e-6
DT_MIN = 1e-3
DT_MAX = 1e-1
A_RE_MAX = -1e-4

kernel_name = "hybrid_pool_s5_parallel_heads"


def rmsnorm(x, gain):
    x32 = x.astype(jnp.float32)
    y = x32 * lax.rsqrt(jnp.mean(x32 * x32, axis=-1, keepdims=True) + EPS)
    return (y * gain.astype(jnp.float32)).astype(x.dtype)


def pool_mixer(u, w_pool, pool_scale):
    B, L, _ = u.shape
    ug = u.astype(jnp.float32).reshape(B, L, N_POOL_GROUPS, POOL_GROUP)
    t = jnp.arange(L)
    outs = []
    for g, w in enumerate(POOL_WINDOWS):
        v = ug[:, :, g]
        cs = jnp.cumsum(v, axis=1)
        lagged = jnp.pad(cs, ((0, 0), (w, 0), (0, 0)))[:, :L]
        count = jnp.minimum(t + 1, w).astype(jnp.float32)[None, :, None]
        outs.append((cs - lagged) / count - v)
    pooled = jnp.stack(outs, axis=2)
    mixed = jnp.einsum('blgc,gcd->blgd', pooled, w_pool.astype(jnp.float32))
    out = mixed.reshape(B, L, POOL_WIDTH) * pool_scale.astype(jnp.float32)
    return out.astype(u.dtype)


def _scan_combine(e1, e2):
    ar1, ai1, br1, bi1 = e1
    ar2, ai2, br2, bi2 = e2
    ar = ar2 * ar1 - ai2 * ai1
    ai = ar2 * ai1 + ai2 * ar1
    br = ar2 * br1 - ai2 * bi1 + br2
    bi = ar2 * bi1 + ai2 * br1 + bi2
    return (ar, ai, br, bi)


def ssm_mixer(u, a_re, a_im, log_dt, b_re, b_im, c_re, c_im, d_skip, w_glu):
    B, L, _ = u.shape
    f32 = jnp.float32
    u32 = u.astype(f32).reshape(B, L, N_SSM_GROUPS, SSM_GROUP)
    lam_re = jnp.minimum(a_re.astype(f32), A_RE_MAX)
    lam_im = a_im.astype(f32)
    dt = jnp.exp(log_dt.astype(f32))[:, None]
    mag = jnp.exp(lam_re * dt)
    ang = lam_im * dt
    ab_re = mag * jnp.cos(ang)
    ab_im = mag * jnp.sin(ang)
    den = lam_re * lam_re + lam_im * lam_im
    n_re = ab_re - 1.0
    n_im = ab_im
    q_re = (n_re * lam_re + n_im * lam_im) / den
    q_im = (n_im * lam_re - n_re * lam_im) / den
    br = b_re.astype(f32)
    bi = b_im.astype(f32)
    bb_re = q_re[..., None] * br - q_im[..., None] * bi
    bb_im = q_re[..., None] * bi + q_im[..., None] * br
    bu_re = jnp.einsum('blgc,gnc->blgn', u32, bb_re)
    bu_im = jnp.einsum('blgc,gnc->blgn', u32, bb_im)
    shp = (1, L, N_SSM_GROUPS, SSM_STATE)
    a_re_t = jnp.broadcast_to(ab_re[None, None], shp)
    a_im_t = jnp.broadcast_to(ab_im[None, None], shp)
    _, _, s_re, s_im = lax.associative_scan(
        _scan_combine, (a_re_t, a_im_t, bu_re, bu_im), axis=1)
    y = (jnp.einsum('blgn,gcn->blgc', s_re, c_re.astype(f32))
         - jnp.einsum('blgn,gcn->blgc', s_im, c_im.astype(f32))
         + d_skip.astype(f32).reshape(N_SSM_GROUPS, SSM_GROUP) * u32)
    y = y.reshape(B, L, SSM_WIDTH)
    g = jax.nn.gelu(y)
    hg = g @ w_glu.astype(f32)
    out = hg[..., :SSM_WIDTH] * jax.nn.sigmoid(hg[..., SSM_WIDTH:])
    return out.astype(u.dtype)


def setup_inputs(seed: int = 0) -> dict:
    key = jax.random.key(seed)
    ks = jax.random.split(key, 20)
    f32 = jnp.float32
    n = lambda k, shape, s: jax.random.normal(k, shape, f32) * s
    x = jax.random.normal(ks[0], (BATCH, SEQ, D_MODEL), f32)
    p = jax.random.normal(ks[1], (DEPTH, BATCH, SEQ, PLE_DIM), f32)
    norm_gain = 1.0 + n(ks[2], (DEPTH, D_MODEL), 0.02)
    w_in = n(ks[3], (DEPTH, D_MODEL, 2 * D_MIX), D_MODEL ** -0.5)
    w_pool = n(ks[4], (DEPTH, N_POOL_GROUPS, POOL_GROUP, POOL_GROUP), POOL_GROUP ** -0.5)
    pool_scale = 1.0 + n(ks[5], (DEPTH, POOL_WIDTH), 0.02)
    a_re = -0.5 + n(ks[6], (DEPTH, N_SSM_GROUPS, SSM_STATE), 0.01)
    a_im = (math.pi * jnp.arange(SSM_STATE, dtype=f32))[None, None, :] + n(
        ks[7], (DEPTH, N_SSM_GROUPS, SSM_STATE), 0.01)
    log_dt = jax.random.uniform(ks[8], (DEPTH, N_SSM_GROUPS), f32,
                                math.log(DT_MIN), math.log(DT_MAX))
    b_scale = (2.0 * SSM_GROUP) ** -0.5
    b_re = n(ks[9], (DEPTH, N_SSM_GROUPS, SSM_STATE, SSM_GROUP), b_scale)
    b_im = n(ks[10], (DEPTH, N_SSM_GROUPS, SSM_STATE, SSM_GROUP), b_scale)
    c_scale = SSM_STATE ** -0.5
    c_re = n(ks[11], (DEPTH, N_SSM_GROUPS, SSM_GROUP, SSM_STATE), c_scale)
    c_im = n(ks[12], (DEPTH, N_SSM_GROUPS, SSM_GROUP, SSM_STATE), c_scale)
    d_skip = n(ks[13], (DEPTH, SSM_WIDTH), 1.0)
    w_glu = n(ks[14], (DEPTH, SSM_WIDTH, 2 * SSM_WIDTH), SSM_WIDTH ** -0.5)
    w_out = n(ks[15], (DEPTH, D_MIX, D_MODEL), D_MIX ** -0.5)
    w_ple = n(ks[16], (DEPTH, PLE_DIM, D_MODEL), PLE_DIM ** -0.5)
    w_ple_gate = n(ks[17], (DEPTH, D_MODEL, D_MODEL), D_MODEL ** -0.5)
    final_gain = 1.0 + n(ks[18], (D_MODEL,), 0.02)
    return {"x": x, "p": p, "norm_gain": norm_gain, "w_in": w_in, "w_pool": w_pool,
            "pool_scale": pool_scale, "a_re": a_re, "a_im": a_im, "log_dt": log_dt,
            "b_re": b_re, "b_im": b_im, "c_re": c_re, "c_im": c_im, "d_skip": d_skip,
            "w_glu": w_glu, "w_out": w_out, "w_ple": w_ple, "w_ple_gate": w_ple_gate,
            "final_gain": final_gain}


def reference(x, p, norm_gain, w_in, w_pool, pool_scale, a_re, a_im, log_dt, b_re, b_im,
              c_re, c_im, d_skip, w_glu, w_out, w_ple, w_ple_gate, final_gain):
    h = x
    for i in range(DEPTH):
        hn = rmsnorm(h, norm_gain[i])
        proj = hn @ w_in[i]
        pool_in = proj[..., :POOL_WIDTH]
        pool_gate = proj[..., POOL_WIDTH:2 * POOL_WIDTH]
        ssm_in = proj[..., 2 * POOL_WIDTH:2 * POOL_WIDTH + SSM_WIDTH]
        ssm_gate = proj[..., 2 * POOL_WIDTH + SSM_WIDTH:]
        ya = pool_mixer(pool_in, w_pool[i], pool_scale[i]) * jax.nn.silu(pool_gate)
        yb = ssm_mixer(ssm_in, a_re[i], a_im[i], log_dt[i], b_re[i], b_im[i],
                       c_re[i], c_im[i], d_skip[i], w_glu[i]) * jax.nn.silu(ssm_gate)
        h = h + jnp.concatenate([ya, yb], axis=-1) @ w_out[i]
        h = h + (p[i] @ w_ple[i]) * jax.nn.sigmoid(h @ w_ple_gate[i])
    return rmsnorm(h, final_gain)
```

```python
import math
from contextlib import ExitStack

import numpy as np
import concourse.bass as bass
import concourse.mybir as mybir
from concourse.bass_utils import run_bass_kernel_spmd

F32 = mybir.dt.float32
BF16 = mybir.dt.bfloat16
I32 = mybir.dt.int32
AF = mybir.ActivationFunctionType
ALU = mybir.AluOpType

D = 2048
NOWN = 2048
NPRE = 2048
NB = 512
TT = NB // 128
MB = NB // 8
QN, PN = 8, 8
_DEBUG_STOP = None
_STAGELOG = None
POOL_W = (2, 4, 8, 16)
TWO_PI = float(2 * math.pi)
PI = float(math.pi)


class V:
    def __init__(self, buf, ap):
        self.buf, self.ap = buf, ap

    @property
    def bufs(self):
        return self.buf if isinstance(self.buf, tuple) else (self.buf,)

    def track(self, *bufs):
        return V(tuple(bufs), self.ap)

    def __getitem__(self, k):
        return V(self.buf, self.ap[k])

    def re(self, pat, **kw):
        return V(self.buf, self.ap.rearrange(pat, **kw))

    def bc(self, shape):
        return V(self.buf, self.ap.to_broadcast(list(shape)))

    def us(self, axis):
        return V(self.buf, self.ap.unsqueeze(axis))

    def cast(self, dt):
        return V(self.buf, self.ap.bitcast(dt))


class Buf:
    def __init__(self, base):
        self.base = base
        self.w = {}
        self.r = {}
        self.dsem = {}
        self.dcnt = {}

    def __getitem__(self, k):
        return V(self, self.base[k])

    def all(self):
        return V(self, self.base[:])


class Eng:
    def __init__(self, k, eng, sem):
        self.k, self.eng, self.sem = k, eng, sem
        self.cnt = 0
        self.known = {}

    def op(self, fn, **kw):
        reads, writes, call = [], [], {}
        for key, val in kw.items():
            if isinstance(val, V):
                (writes if key in ("out", "accum_out") else reads).extend(val.bufs)
                call[key] = val.ap
            else:
                call[key] = val
        self.k.sync(self, reads, writes)
        ins = getattr(self.eng, fn)(**call)
        self.k.done(self, ins, reads, writes)

    def memset(self, view, val):
        self.k.sync(self, [], list(view.bufs))
        ins = self.eng.memset(view.ap, val)
        self.k.done(self, ins, [], list(view.bufs))


class Kern:
    def __init__(self, nc, es):
        self.nc, self.es = nc, es
        self.nsem = 0
        self.n_mm = 0
        self.bufs = []
        self.pe = Eng(self, nc.tensor, self.sem("pe"))
        self.act = Eng(self, nc.scalar, self.sem("act"))
        self.dve = Eng(self, nc.vector, self.sem("dve"))
        self.sq = Eng(self, nc.sync, None)
        self.gq = Eng(self, nc.gpsimd, None)
        self.engs = [self.pe, self.act, self.dve, self.sq, self.gq]

    def sem(self, name):
        self.nsem += 1
        return self.es.enter_context(self.nc.semaphore(f"s{self.nsem}_{name}"))

    def sb(self, name, shape, dt):
        b = Buf(self.es.enter_context(self.nc.sbuf_tensor("b_" + name, list(shape), dt)))
        self.bufs.append(b)
        return b

    def ps(self, name, shape, dt):
        b = Buf(self.es.enter_context(self.nc.psum_tensor("q_" + name, list(shape), dt)))
        self.bufs.append(b)
        return b

    def dram(self, name, shape, dt):
        b = Buf(self.nc.dram_tensor(name, list(shape), dt).ap())
        self.bufs.append(b)
        return b

    def sync(self, E, reads, writes):
        deps = {}

        def upd(s, v):
            if deps.get(s, (None, 0))[1] < v:
                deps[s] = (s, v)

        for b in reads:
            for s, (so, v) in b.w.items():
                upd_key(deps, so, v)
        for b in writes:
            for s, (so, v) in b.w.items():
                if so is not E.sem or E is not self.pe:
                    upd_key(deps, so, v)
            for s, (so, v) in b.r.items():
                if so is not E.sem or E is not self.pe:
                    upd_key(deps, so, v)
        for key, (so, v) in deps.items():
            if E.known.get(key, 0) < v:
                E.eng.wait_ge(so, v)
                E.known[key] = v

    def done(self, E, ins, reads, writes, dma_buf=None):
        if dma_buf is not None:
            if E not in dma_buf.dsem:
                dma_buf.dsem[E] = self.sem("d")
                dma_buf.dcnt[E] = 0
            dma_buf.dcnt[E] += 16
            ins.then_inc(dma_buf.dsem[E], 16)
            so, v = dma_buf.dsem[E], dma_buf.dcnt[E]
        else:
            E.cnt += 1
            ins.then_inc(E.sem, 1)
            so, v = E.sem, E.cnt
        key = id(so)
        for b in reads:
            if b.r.get(key, (None, 0))[1] < v:
                b.r[key] = (so, v)
        for b in writes:
            b.r = {}
            b.w[key] = (so, v)

    def dma(self, Q, out, in_, **kw):
        reads = list(in_.bufs) if isinstance(in_, V) else []
        writes = list(out.bufs) if isinstance(out, V) else []
        self.sync(Q, reads, writes)
        oa = out.ap if isinstance(out, V) else out
        ia = in_.ap if isinstance(in_, V) else in_
        ins = Q.eng.dma_start(out=oa, in_=ia, **kw)
        db = (writes + reads)[0]
        self.done(Q, ins, reads, writes, dma_buf=db)

    def mm(self, mms, transpose=False):
        reads, writes = [], []
        for m in mms:
            writes.extend(m["out"].bufs)
            reads.extend(m["a"].bufs)
            reads.extend(m["b"].bufs)
        self.sync(self.pe, reads, writes)
        self.n_mm += len(mms)
        ins = None
        for m in mms:
            if transpose:
                ins = self.nc.tensor.transpose(m["out"].ap, m["a"].ap, m["b"].ap)
            else:
                kw = {}
                if m.get("tp") is not None:
                    kw["tile_position"] = m["tp"]
                ins = self.nc.tensor.matmul(m["out"].ap, lhsT=m["a"].ap, rhs=m["b"].ap,
                                            start=m.get("start", True), stop=m.get("stop", True), **kw)
        self.done(self.pe, ins, reads, writes)

    def barrier(self, skip=()):
        evs = {}
        for E in self.engs:
            if E.sem is not None and E.cnt > 0:
                evs[id(E.sem)] = (E.sem, E.cnt)
        for b in self.bufs:
            for Q, sm in b.dsem.items():
                evs[id(sm)] = (sm, b.dcnt[Q])
        for E in self.engs:
            if E in skip:
                continue
            for key, (so, v) in evs.items():
                if so is E.sem:
                    continue
                if E.known.get(key, 0) < v:
                    E.eng.wait_ge(so, v)
                    E.known[key] = v


def upd_key(deps, so, v):
    key = id(so)
    if deps.get(key, (None, 0))[1] < v:
        deps[key] = (so, v)


def build():
    nc = bass.Bass("TRN2", target_bir_lowering=False)

    def din(name, shape):
        return nc.dram_tensor(name, list(shape), F32, kind="ExternalInput").ap()

    x_own = din("x_own", [NOWN, D])
    x_pre = din("x_pre", [NPRE, D])
    p_own = din("p_own", [NOWN, 256])
    norm_gain = din("norm_gain", [D])
    w_in = din("w_in", [D, 4096])
    w_pool = din("w_pool", [4, 256, 256])
    pool_scale = din("pool_scale", [1024])
    a_re = din("a_re", [64, 64])
    a_im = din("a_im", [64, 64])
    log_dt = din("log_dt", [64])
    b_re = din("b_re", [64, 64, 16])
    b_im = din("b_im", [64, 64, 16])
    c_re = din("c_re", [64, 16, 64])
    c_im = din("c_im", [64, 16, 64])
    d_skip = din("d_skip", [1024])
    w_glu = din("w_glu", [1024, 2048])
    w_out = din("w_out", [D, D])
    w_ple = din("w_ple", [256, D])
    w_ple_gate = din("w_ple_gate", [D, D])
    final_gain = din("final_gain", [D])
    ident_d = din("ident", [128, 128])
    mask_d = din("mask", [128, 128])
    cf_d = din("cf", [128, 64])
    out_d = nc.dram_tensor("out", [NOWN, D], F32, kind="ExternalOutput").ap()

    with ExitStack() as es:
        K = Kern(nc, es)
        pe, act, dve, sq, gq = K.pe, K.act, K.dve, K.sq, K.gq

        WS = [K.sb(f"ws{i}", [128, 8192], BF16) for i in range(2)]
        SW = [K.sb(f"sw{i}", [128, 3072], BF16) for i in range(2)]
        XS = [K.sb(f"xs{i}", [128, 2048], F32) for i in range(2)]
        HNT = K.sb("hnt", [128, 8192], BF16)
        XNS = [K.sb(f"xn{i}", [128, 2048], BF16) for i in range(2)]
        XN = XNS[0]
        AR1 = K.sb("ar1", [128, 8768], F32)
        U = K.sb("u", [128, 4096], BF16)
        SG = K.sb("sg", [128, 4096], BF16)
        YA = K.sb("ya", [128, 4096], BF16)
        G = K.sb("g", [128, 4096], BF16)
        ZB = K.sb("zb", [128, 4096], BF16)
        PWT = K.sb("pwt", [128, 2 * 528], F32)
        HALO = K.sb("halo", [128, 8 * 16], BF16)
        ZE = [K.sb(f"ze{i}", [128, 64 * 9], F32) for i in range(2)]
        PTB = K.sb("ptb", [128, 2 * NB], BF16)
        PST = [K.sb(f"pst{i}", [128, 256], BF16) for i in range(2)]
        SGS = [K.sb(f"sgs{i}", [128, 512], F32) for i in range(2)]
        FG = K.sb("fg", [128, 2048], F32)
        WPL = K.sb("wpl", [128, 2 * 4 * 2 * 256], BF16)
        IDF = K.sb("idf", [128, 128], F32)
        IDB = K.sb("idb", [128, 128], BF16)
        MASK = K.sb("mask", [128, 128], F32)
        GT = K.sb("gt", [128, 16], F32)
        PSC = K.sb("psc", [128, 8], F32)
        DSK = K.sb("dsk", [128, 8], F32)
        CF = K.sb("cf", [128, 64], F32)
        SCN = K.sb("scn", [128, 1920], F32)
        SS = K.sb("ss", [128, 8], F32)
        EPS = K.sb("eps", [128, 2], F32)
        KI = K.sb("ki", [128, 32], I32)
        PF = [K.ps(f"pf{i}", [128, 512], F32) for i in range(6)]
        PT = [K.ps(f"pt{i}", [128, 1024], BF16) for i in range(2)]
        WU_d = K.dram("wu_d", [8, 128, 2048], BF16)
        WD_d = K.dram("wd_d", [8, 128, 3072], BF16)

        pfi = [0]

        def pf():
            b = PF[pfi[0] % 6]
            pfi[0] += 1
            return b

        def tt_(E, out, a, b, op):
            E.op("tensor_tensor", out=out, in0=a, in1=b, op=op)

        def ts_(E, out, a, s1, op0, s2=None, op1=None):
            if op1 is None:
                E.op("tensor_scalar", out=out, in0=a, scalar1=s1, scalar2=None, op0=op0)
            else:
                E.op("tensor_scalar", out=out, in0=a, scalar1=s1, scalar2=s2, op0=op0, op1=op1)

        def stt_(E, out, a, s, b, op0, op1):
            E.op("scalar_tensor_tensor", out=out, in0=a, scalar=s, in1=b, op0=op0, op1=op1)

        def cp_(E, out, a):
            E.op("tensor_copy", out=out, in_=a)

        def actf(out, a, func, **kw):
            act.op("activation", out=out, in_=a, func=func, **kw)

        K.dma(sq, IDF.all(), ident_d)
        K.dma(sq, MASK.all(), mask_d)
        K.dma(sq, CF.all(), cf_d)
        K.dma(sq, FG.all(), final_gain.partition_broadcast(128))
        K.dma(sq, AR1[0:16, 4096:4224], norm_gain.rearrange("(k p) -> k p", p=128))
        K.dma(sq, AR1[0:8, 4224:4352], pool_scale.rearrange("(k p) -> k p", p=128))
        K.dma(sq, AR1[0:8, 4352:4480], d_skip.rearrange("(k p) -> k p", p=128))
        K.dma(sq, AR1[0:32, 4480:4608], a_re.rearrange("(j t) n -> j (t n)", t=2))
        K.dma(sq, AR1[0:32, 4608:4736], a_im.rearrange("(j t) n -> j (t n)", t=2))
        K.dma(sq, AR1[0:32, 4864:4866], log_dt.rearrange("(j t) -> j t", t=2))
        K.dma(sq, AR1[0:32, 0:2048], b_re.rearrange("(j t) n c -> j (t n c)", t=2))
        K.dma(sq, AR1[0:32, 2048:4096], b_im.rearrange("(j t) n c -> j (t n c)", t=2))
        cp_(dve, AR1[0:32, 4736:4864].re("p (t n) -> p t n", t=2), AR1[0:32, 4864:4866].us(2).bc([32, 2, 64]))
        cp_(dve, IDB.all(), IDF.all())
        bank0 = pf()
        smalls = [(16, 4096, 0), (8, 4224, 16), (8, 4352, 24), (32, 4480, 32), (32, 4608, 64), (32, 4736, 96)]
        K.mm([dict(out=bank0[:, co:co + kk], a=AR1[0:kk, c0:c0 + 128], b=IDF[0:kk, 0:kk]) for kk, c0, co in smalls],
             transpose=True)
        cp_(dve, GT.all(), bank0[:, 0:16])
        cp_(dve, PSC.all(), bank0[:, 16:24])
        cp_(dve, DSK.all(), bank0[:, 24:32])
        cp_(dve, XS[0][:, 0:96], bank0[:, 32:128])
        dve.memset(EPS.all(), 1e-6)
        WPLv = WPL.all().re("p (v g c d) -> p v g c d", v=2, g=4, c=2)
        K.dma(gq, WPLv[:, 1], w_pool.rearrange("g (c p) d -> p g c d", p=128))
        for g in range(4):
            ts_(dve, WPLv[:, 0, g], WPLv[:, 1, g], 1.0 / POOL_W[g], ALU.mult)
        ts_(dve, WPLv[:, 1], WPLv[:, 1], -1.0, ALU.mult)

        if _DEBUG_STOP == 10:
            K.barrier()
            return nc
        SM = XS[0]

        def sl(i):
            return SM[:, 32 * i:32 * i + 32]

        are, aim, ldt = sl(0), sl(1), sl(2)
        lre, dt_, xr, mag, ang = sl(3), sl(4), sl(5), sl(6), sl(7)
        ts_(dve, lre, are, -1e-4, ALU.min)
        actf(dt_, ldt, AF.Exp)
        tt_(dve, xr, lre, dt_, ALU.mult)
        actf(mag, xr, AF.Exp)
        tt_(dve, ang, aim, dt_, ALU.mult)

        def range_reduce(dst, src, addc):
            t, kf, m = sl(8), sl(9), sl(10)
            ts_(dve, t, src, addc, ALU.add)
            ts_(dve, KI.all(), t, 1.0 / TWO_PI, ALU.mult)
            cp_(dve, kf, KI.all())
            stt_(dve, dst, kf, -TWO_PI, t, ALU.mult, ALU.add)
            ts_(dve, m, dst, PI, ALU.is_gt, TWO_PI, ALU.mult)
            tt_(dve, dst, dst, m, ALU.subtract)
            ts_(dve, m, dst, -PI, ALU.is_lt, TWO_PI, ALU.mult)
            tt_(dve, dst, dst, m, ALU.add)

        rs, rc, sn, cs = sl(11), sl(12), sl(13), sl(14)
        range_reduce(rs, ang, 0.0)
        range_reduce(rc, ang, PI / 2)
        actf(sn, rs, AF.Sin)
        actf(cs, rc, AF.Sin)
        abre, abim = sl(26), sl(27)
        tt_(dve, abre, mag, cs, ALU.mult)
        tt_(dve, abim, mag, sn, ALU.mult)
        den, rden, nre, t1, t2, qre, qim = sl(15), sl(16), sl(17), sl(18), sl(19), sl(20), sl(21)
        tt_(dve, den, lre, lre, ALU.mult)
        tt_(dve, t1, aim, aim, ALU.mult)
        tt_(dve, den, den, t1, ALU.add)
        dve.op("reciprocal", out=rden, in_=den)
        ts_(dve, nre, abre, -1.0, ALU.add)
        tt_(dve, t1, nre, lre, ALU.mult)
        tt_(dve, t2, abim, aim, ALU.mult)
        tt_(dve, t1, t1, t2, ALU.add)
        tt_(dve, qre, t1, rden, ALU.mult)
        tt_(dve, t1, abim, lre, ALU.mult)
        tt_(dve, t2, nre, aim, ALU.mult)
        tt_(dve, t1, t1, t2, ALU.subtract)
        tt_(dve, qim, t1, rden, ALU.mult)

        def cmul(ore, oim, xre, xim, yre, yim, tA, tB):
            tt_(dve, tA, xre, yre, ALU.mult)
            tt_(dve, tB, xim, yim, ALU.mult)
            tt_(dve, ore, tA, tB, ALU.subtract)
            tt_(dve, tA, xre, yim, ALU.mult)
            tt_(dve, tB, xim, yre, ALU.mult)
            tt_(dve, oim, tA, tB, ALU.add)

        def Pre(k):
            return sl(24 + 2 * k)

        def Pim(k):
            return sl(25 + 2 * k)

        dve.memset(Pre(0), 1.0)
        dve.memset(Pim(0), 0.0)
        for k in range(1, 8):
            cmul(Pre(k + 1), Pim(k + 1), Pre(k), Pim(k), abre, abim, sl(22), sl(23))
        Ar, Ai = Pre(8), Pim(8)

        def APre(p):
            return sl(44 + 2 * p)

        def APim(p):
            return sl(45 + 2 * p)

        dve.memset(APre(0), 1.0)
        dve.memset(APim(0), 0.0)
        for p in range(0, 8):
            cmul(APre(p + 1), APim(p + 1), APre(p), APim(p), Ar, Ai, sl(22), sl(23))
        Ar2 = SCN[:, 0:64]
        Ai_t = SCN[:, 64:96]
        nAi_t = SCN[:, 96:128]
        AQr2 = SCN[:, 128:192]
        AQi_t = SCN[:, 192:224]
        nAQi_t = SCN[:, 224:256]
        APr2 = SCN[:, 256:768]
        APi_t = SCN[:, 768:1024]
        nAPi_t = SCN[:, 1024:1280]
        AiS2 = SCN[:, 1280:1344]
        AQiS2 = SCN[:, 1344:1408]
        cp_(dve, Ar2.re("p (j t) -> p j t", t=2), Ar.us(2).bc([128, 32, 2]))
        ts_(dve, AiS2.re("p (j t) -> p j t", t=2)[:, :, 0], Ai, -1.0, ALU.mult)
        cp_(dve, AiS2.re("p (j t) -> p j t", t=2)[:, :, 1], Ai)
        ts_(dve, AQiS2.re("p (j t) -> p j t", t=2)[:, :, 0], APim(8), -1.0, ALU.mult)
        cp_(dve, AQiS2.re("p (j t) -> p j t", t=2)[:, :, 1], APim(8))
        cp_(dve, Ai_t, Ai)
        ts_(dve, nAi_t, Ai, -1.0, ALU.mult)
        cp_(dve, AQr2.re("p (j t) -> p j t", t=2), APre(8).us(2).bc([128, 32, 2]))
        cp_(dve, AQi_t, APim(8))
        ts_(dve, nAQi_t, APim(8), -1.0, ALU.mult)
        for p in range(8):
            cp_(dve, APr2.re("p (j t q) -> p j t q", t=2, q=8)[:, :, :, p], APre(p).us(2).bc([128, 32, 2]))
            cp_(dve, APi_t.re("p (j q) -> p j q", q=8)[:, :, p], APim(p))
            ts_(dve, nAPi_t.re("p (j q) -> p j q", q=8)[:, :, p], APim(p), -1.0, ALU.mult)

        if _DEBUG_STOP == 11:
            K.barrier()
            return nc
        QTr, QTi = [], []
        qre_s, qim_s = APre(8), APim(8)
        for kq in range(4):
            tr = SCN[:, 1408 + 128 * kq:1408 + 128 * kq + 64]
            ti = SCN[:, 1408 + 128 * kq + 64:1408 + 128 * kq + 128]
            cp_(dve, tr.re("p (j t) -> p j t", t=2), qre_s.us(2).bc([128, 32, 2]))
            ts_(dve, ti.re("p (j t) -> p j t", t=2)[:, :, 0], qim_s, -1.0, ALU.mult)
            cp_(dve, ti.re("p (j t) -> p j t", t=2)[:, :, 1], qim_s)
            QTr.append(tr)
            QTi.append(ti)
            if kq < 3:
                nre_, nim_ = sl(5 + 2 * kq), sl(6 + 2 * kq)
                cmul(nre_, nim_, qre_s, qim_s, qre_s, qim_s, sl(22), sl(23))
                qre_s, qim_s = nre_, nim_
        XB = XS[1]

        def xb(i):
            return XB[:, 512 * i:512 * i + 512].re("p (j c) -> p j c", c=16)

        for part in range(2):
            bank = pf()
            K.mm([dict(out=bank[:, 32 * c:32 * c + 32],
                       a=AR1[0:32, 2048 * part:2048 * part + 2048].re("p (m c) -> p m c", c=16)[:, :, c],
                       b=IDF[0:32, 0:32]) for c in range(16)], transpose=True)
            cp_(dve, xb(part), bank.all().re("p (c j) -> p j c", c=16))
        TF = HNT.all().cast(F32)

        def tf(i):
            return TF[:, 512 * i:512 * i + 512].re("p (j c) -> p j c", c=16)

        def b16(v):
            return v.us(2).bc([128, 32, 16])

        cmul(xb(2), xb(3), xb(0), xb(1), b16(qre), b16(qim), tf(0), tf(1))

        if _DEBUG_STOP == 12:
            K.barrier()
            return nc
        CB = [AR1[0:32, 0:4096].re("p (j n) -> p j n", n=128), AR1[0:32, 4096:8192].re("p (j n) -> p j n", n=128)]
        dve.memset(AR1[0:32, 0:8192], 0.0)
        for g2 in range(2):
            for part, csrc in enumerate((c_re, c_im)):
                K.dma(sq, CB[part][16 * g2:16 * g2 + 16, :, 64 * g2:64 * g2 + 64],
                      csrc.rearrange("(j t) c n -> t c j n", t=2)[g2])
        SGF = SG.all().cast(F32)
        GF = G.all().cast(F32)
        CST = [SGF[:, 0:1024], SGF[:, 1024:2048]]
        NCSTI = GF[:, 0:1024]
        for part in range(2):
            for hb in range(2):
                bank = pf()
                K.mm([dict(out=bank[:, 32 * jj:32 * jj + 32], a=CB[part][:, 16 * hb + jj, :], b=IDF[0:32, 0:32])
                      for jj in range(16)], transpose=True)
                actf(CST[part][:, 512 * hb:512 * hb + 512], bank.all(), AF.Copy)
        ts_(dve, NCSTI, CST[1], -1.0, ALU.mult)

        def c3(v):
            return v.re("p (j c) -> p j c", c=32)

        if _DEBUG_STOP == 13:
            K.barrier()
            return nc
        WC_dv = WD_d[:, :, 0:2048].re("J p (r j4 tc) -> p J r j4 tc", r=8, j4=4)
        for r in range(8):
            pr = Pre(r + 1).us(2).bc([128, 32, 32])
            pi = Pim(r + 1).us(2).bc([128, 32, 32])
            ta, tb = c3(TF[:, 0:1024]), c3(TF[:, 1024:2048])
            WCraw = WS[r % 2][:, 4096:6144]
            WCst = WCraw.re("p (j t c) -> p j t c", t=2, c=32)
            tt_(dve, ta, c3(CST[0]), pr, ALU.mult)
            tt_(dve, tb, c3(CST[1]), pi, ALU.mult)
            tt_(dve, WCst[:, :, 0, :], ta, tb, ALU.subtract)
            tt_(dve, ta, c3(NCSTI), pr, ALU.mult)
            tt_(dve, tb, c3(CST[0]), pi, ALU.mult)
            tt_(dve, WCst[:, :, 1, :], ta, tb, ALU.subtract)
            K.dma(sq, WC_dv[:, :, r], WCraw.re("p (J j4 tc) -> p J j4 tc", J=8, j4=4))

        if _DEBUG_STOP == 14:
            K.barrier()
            return nc
        UF = U.all().cast(F32)
        SD = [UF[:, 0:1024].re("p (j c) -> p j c", c=32), UF[:, 1024:2048].re("p (j c) -> p j c", c=32)]
        dve.memset(UF, 0.0)
        WU_dv = WU_d.all().re("J p (r t c) -> p J r t c", r=8, t=2)
        KD_dv = WD_d[:, :, 2048:3072].re("J p (d c) -> p J d c", d=8)
        TK = TF[:, 2048:3072].re("p (J c) -> p J c", c=128)
        for d in range(8):
            r = 7 - d
            WUraw = WS[d % 2][:, 0:2048]
            WUst = WUraw.re("p (J t c) -> p J t c", t=2, c=128)
            KDst = WS[d % 2][:, 2048:3072].re("p (J c) -> p J c", c=128)
            cmul(tf(2), tf(3), xb(2), xb(3), b16(Pre(d)), b16(Pim(d)), tf(0), tf(1))
            for g2 in range(2):
                ps_ = slice(64 * g2, 64 * g2 + 64)
                cp_(dve, SD[0][ps_, :, 16 * g2:16 * g2 + 16], tf(2)[ps_])
                cp_(dve, SD[1][ps_, :, 16 * g2:16 * g2 + 16], tf(3)[ps_])
            for bq in range(4):
                bank = pf()
                ms = []
                for ii in range(4):
                    J, part = divmod(4 * bq + ii, 2)
                    ms.append(dict(out=bank[:, 128 * ii:128 * ii + 128],
                                   a=UF[:, 1024 * part + 128 * J:1024 * part + 128 * J + 128], b=IDF.all()))
                K.mm(ms, transpose=True)
                actf(WUraw[:, 512 * bq:512 * bq + 512], bank.all(), AF.Copy)
            K.dma(sq, WU_dv[:, :, r], WUst)
            for hb in range(2):
                bank = pf()
                ms = []
                for ii in range(4):
                    J = 4 * hb + ii
                    o = bank[:, 128 * ii:128 * ii + 128]
                    ms.append(dict(out=o, a=UF[:, 128 * J:128 * J + 128], b=CST[0][:, 128 * J:128 * J + 128],
                                   start=True, stop=False))
                    ms.append(dict(out=o, a=UF[:, 1024 + 128 * J:1024 + 128 * J + 128], b=NCSTI[:, 128 * J:128 * J + 128],
                                   start=False, stop=True))
                K.mm(ms)
                bv = bank.all().re("p (J c) -> p J c", c=128)
                mk = MASK.all().us(1).bc([128, 4, 128])
                if d == 0:
                    tt_(dve, TK[:, 4 * hb:4 * hb + 4, :], bv, mk, ALU.mult)
                    for ii in range(4):
                        J = 4 * hb + ii
                        stt_(dve, KDst[:, J, :], IDF.all(), DSK[:, J:J + 1], TK[:, J, :], ALU.mult, ALU.add)
                else:
                    tt_(dve, KDst[:, 4 * hb:4 * hb + 4, :], bv, mk, ALU.mult)
            K.dma(sq, KD_dv[:, :, d], KDst)

        K.barrier(skip=(gq,))
        if _DEBUG_STOP == 1:
            return nc

        HNTv = HNT.all().re("p (k t) -> p k t", k=16)
        R_PIN, R_PG, R_VX = Buf(None), Buf(None), Buf(None)
        K.bufs += [R_PIN, R_PG, R_VX]
        PINall = AR1[:, 0:2112].cast(BF16).re("p (i c) -> p i c", c=528).track(R_PIN)
        PGall = AR1[:, 2112:4160].cast(BF16).re("p (i c) -> p i c", c=512).track(R_PG)
        VX = AR1[:, 4160:8768].re("p (jt q c) -> p jt q c", q=QN, c=PN + 1).track(R_VX)
        VX5 = AR1[:, 4160:8768].re("p (j t q c) -> p j t q c", t=2, q=QN, c=PN + 1).track(R_VX)
        Hreg = [(R_PIN,), (R_PIN, R_PG), (R_PG, R_VX), (R_VX,)]
        Hs = [AR1[:, 2048 * t:2048 * t + 2048].track(*Hreg[t]) for t in range(TT)]
        Uv = U.all().re("p (i c) -> p i c", c=512)
        SGv = SG.all().re("p (i c) -> p i c", c=512)
        YAv = YA.all().re("p (i c) -> p i c", c=512)
        YBv = ZB.all().re("p (i c) -> p i c", c=512)
        Gv = G.all().re("p (i c) -> p i c", c=512)
        ZBv = ZB.all().re("p (j t m) -> p j t m", t=2, m=MB)
        HALOv = HALO.all().re("p (i c) -> p i c", c=16)
        PTBv = PTB.all().re("p (k t) -> p k t", k=2)
        TH = [YA.all().cast(F32), G.all().cast(F32)]
        T1 = TH[0]
        w_in_v = w_in.rearrange("(k p) c -> p k c", p=128)
        w_glu_v = w_glu.rearrange("(k p) c -> p k c", p=128)
        w_out_v = w_out.rearrange("(k p) c -> p k c", p=128)
        w_pg_v = w_ple_gate.rearrange("(k p) c -> p k c", p=128)
        w_ple_v = w_ple.rearrange("(k p) c -> p k c", p=128)
        wsi = [0]
        swi = [0]

        def ws():
            b = WS[wsi[0] % 2]
            wsi[0] += 1
            return b

        def sw():
            b = SW[swi[0] % 2]
            swi[0] += 1
            return b

        wcache = {}
        pend = []

        def flush_store():
            while pend:
                key, slot = pend.pop(0)
                K.dma(gq, wcache[key].all(), slot.all())

        def load_w(key, slot, loads):
            if key in wcache:
                flush_store()
                K.dma(gq, slot.all(), wcache[key].all())
                return
            for dst, src in loads:
                K.dma(gq, dst, src)
            flush_store()
            wcache[key] = K.dram("wsc_" + key, [128, 8192], BF16)
            pend.append((key, slot))

        def rms_stats(src, junk):
            actf(junk.all(), src, AF.Square, accum_out=SS[:, 0:1])
            actf(SS[:, 1:2], SS[:, 0:1], AF.Sqrt, bias=EPS[:, 0:1], scale=1.0 / D)
            dve.op("reciprocal", out=SS[:, 2:3], in_=SS[:, 1:2])

        def stage0(xsrc, row0, tiles=None):
            for t in (range(TT) if tiles is None else tiles):
                xs = XS[t % 2]
                K.dma(sq, xs.all(), xsrc[row0 + 128 * t:row0 + 128 * t + 128, :])
                XN = XNS[t % 2]
                rms_stats(xs.all(), XN)
                ts_(dve, XN.all(), xs.all(), SS[:, 2:3], ALU.mult)
                for hk in range(2):
                    K.mm([dict(out=PT[hk][:, 128 * kk:128 * kk + 128],
                               a=XN[:, 128 * (8 * hk + kk):128 * (8 * hk + kk) + 128], b=IDB.all())
                          for kk in range(8)], transpose=True)
                    tt_(dve, HNTv[:, 8 * hk:8 * hk + 8, 128 * t:128 * t + 128],
                        PT[hk].all().re("p (k t) -> p k t", k=8),
                        GT[:, 8 * hk:8 * hk + 8].us(2).bc([128, 8, 128]), ALU.mult)

        def stage1(chunks, uview=None):
            uview = Uv if uview is None else uview
            for ci in chunks:
                wb = ws()
                wv = wb.all().re("p (k c) -> p k c", k=16)
                load_w(f"win{ci}", wb, [(wv, w_in_v[:, :, 512 * ci:512 * ci + 512])])
                for cc in range(4):
                    c = 4 * ci + cc
                    grp, idx = divmod(c, 8)
                    bank = pf()
                    K.mm([dict(out=bank.all(), a=wv[:, k, 128 * cc:128 * cc + 128], b=HNTv[:, k, :],
                               start=(k == 0), stop=(k == 15)) for k in range(16)])
                    if grp == 0:
                        actf(PINall[:, idx, 16:528], bank.all(), AF.Copy)
                    elif grp == 1:
                        actf(PGall[:, idx, :], bank.all(), AF.Silu)
                    elif grp == 2:
                        actf(uview[:, idx, :], bank.all(), AF.Copy)
                    else:
                        actf(SGv[:, idx, :], bank.all(), AF.Silu)

        def stage2(first_own):
            PA = PWT[:, 0:528]
            PB = PWT[:, 528:1056]
            for i in range(8):
                g4 = i // 2
                w = POOL_W[g4]
                a1 = PINall[:, i, :]
                tt_(dve, PA[:, 1:528], a1[:, 1:528], a1[:, 0:527], ALU.add)
                cur = PA
                if w >= 4:
                    tt_(dve, PB[:, 3:528], PA[:, 3:528], PA[:, 1:526], ALU.add)
                    cur = PB
                if w >= 8:
                    tt_(dve, PA[:, 7:528], PB[:, 7:528], PB[:, 3:524], ALU.add)
                    cur = PA
                if w >= 16:
                    tt_(dve, PB[:, 15:528], PA[:, 15:528], PA[:, 7:520], ALU.add)
                    cur = PB
                if first_own:
                    tt_(dve, cur[:, 16:32], cur[:, 16:32], CF[:, 16 * g4:16 * g4 + 16], ALU.mult)
                cp_(dve, Gv[:, i, :], cur[:, 16:528])
            cp_(dve, HALOv, PINall[:, :, 512:528])
            for g4 in range(4):
                for co in range(2):
                    io = 2 * g4 + co
                    bank = pf()
                    ms = []
                    for c2 in range(2):
                        ms.append(dict(out=bank.all(), a=WPLv[:, 0, g4, c2, 128 * co:128 * co + 128],
                                       b=Gv[:, 2 * g4 + c2, :], start=(c2 == 0), stop=False))
                        ms.append(dict(out=bank.all(), a=WPLv[:, 1, g4, c2, 128 * co:128 * co + 128],
                                       b=PINall[:, 2 * g4 + c2, 16:528], start=False, stop=(c2 == 1)))
                    K.mm(ms)
                    stt_(dve, YAv[:, io, :], bank.all(), PSC[:, io:io + 1], PGall[:, io, :], ALU.mult, ALU.mult)

        def ssm_up(Jgs=(0, 1), uview=None):
            uview = Uv if uview is None else uview
            if 0 in Jgs:
                dve.memset(VX[:, :, :, 0:1], 0.0)
            for Jg in Jgs:
                banks = [pf() for _ in range(4)]
                for Jl in range(4):
                    J = 4 * Jg + Jl
                    sb_ = sw()
                    K.dma(gq, sb_[:, 0:2048], WU_d[J])
                    wv = sb_[:, 0:2048].re("p (r t c) -> p r t c", r=8, t=2)
                    ms = []
                    for part in range(2):
                        for r in range(8):
                            for j4 in range(4):
                                rows = slice(32 * j4, 32 * j4 + 32)
                                o = banks[j4][:, 128 * Jl + 64 * part:128 * Jl + 64 * part + 64]
                                ms.append(dict(out=o, a=wv[rows, r, part, :], b=uview[rows, J, r::8],
                                               start=(r == 0), stop=(r == 7), tp=(32 * j4, 0)))
                    K.mm(ms)
                for j4 in range(4):
                    for Jl in range(4):
                        J = 4 * Jg + Jl
                        actf(VX[:, 8 * J + 2 * j4:8 * J + 2 * j4 + 2, :, 1:PN + 1],
                             banks[j4][:, 128 * Jl:128 * Jl + 128].re("p (t q c) -> p t q c", t=2, q=QN), AF.Copy)

        def cstep(X64, Xsw, Ar2v, AiSv, ta, tb):
            tt_(dve, ta, X64, Ar2v, ALU.mult)
            tt_(dve, tb, Xsw, AiSv, ALU.mult)

        def ssm_scan(ze_in, ze_out, fixup):
            ta = T1[:, 0:512].re("p (a q) -> p a q", q=QN)
            tb = T1[:, 512:1024].re("p (j t q) -> p j t q", t=2, q=QN)
            tbf = T1[:, 512:1024].re("p (a q) -> p a q", q=QN)
            for c in range(2, PN + 1):
                cstep(VX[:, :, :, c - 1], VX5[:, :, ::-1, :, c - 1],
                      Ar2.us(2).bc([128, 64, QN]), AiS2.re("p (j t) -> p j t", t=2).us(3).bc([128, 32, 2, QN]), ta, tb)
                tt_(dve, VX[:, :, :, c], VX[:, :, :, c], ta, ALU.add)
                tt_(dve, VX[:, :, :, c], VX[:, :, :, c], tbf, ALU.add)
            Z = ze_out.all().re("p (a q) -> p a q", q=9)
            Z4 = ze_out.all().re("p (j t q) -> p j t q", t=2, q=9)
            cp_(dve, Z[:, :, 0], ze_in.all().re("p (a q) -> p a q", q=9)[:, :, QN])
            cp_(dve, Z[:, :, 1:QN + 1], VX[:, :, :, PN])
            for kq, sft in enumerate((1, 2, 4, 8)):
                n = QN + 1 - sft
                sa = T1[:, 1024:1024 + 64 * n].re("p (a q) -> p a q", q=n)
                sbv = T1[:, 1536:1536 + 64 * n].re("p (j t q) -> p j t q", t=2, q=n)
                sbf = T1[:, 1536:1536 + 64 * n].re("p (a q) -> p a q", q=n)
                tt_(dve, sa, Z[:, :, 0:n], QTr[kq].us(2).bc([128, 64, n]), ALU.mult)
                tt_(dve, sbv, Z4[:, :, ::-1, 0:n], QTi[kq].re("p (j t) -> p j t", t=2).us(3).bc([128, 32, 2, n]), ALU.mult)
                tt_(dve, Z[:, :, sft:QN + 1], Z[:, :, sft:QN + 1], sa, ALU.add)
                tt_(dve, Z[:, :, sft:QN + 1], Z[:, :, sft:QN + 1], sbf, ALU.add)
            if not fixup:
                return
            zb5 = ZB.all().re("p (j t q c) -> p j t q c", t=2, q=QN, c=PN)
            for hh in range(2):
                ja, jb = 32 * hh, 32 * hh + 32
                ka, kb = 16 * hh, 16 * hh + 16
                tfh = TH[hh].re("p (a q c) -> p a q c", q=QN, c=PN)
                tt_(dve, tfh, APr2.re("p (a c) -> p a c", c=PN)[:, ja:jb].us(2).bc([128, 32, QN, PN]),
                    Z[:, ja:jb, 0:QN].us(3).bc([128, 32, QN, PN]), ALU.mult)
                tt_(dve, tfh, tfh, VX[:, ja:jb, :, 0:PN], ALU.add)
                x0 = XS[0][:, 1024 * hh:1024 * hh + 1024].re("p (j q c) -> p j q c", q=QN, c=PN)
                x1 = XS[1][:, 1024 * hh:1024 * hh + 1024].re("p (j q c) -> p j q c", q=QN, c=PN)
                tt_(dve, x0, nAPi_t.re("p (j c) -> p j c", c=PN)[:, ka:kb].us(2).bc([128, 16, QN, PN]),
                    Z4[:, ka:kb, 1, 0:QN].us(3).bc([128, 16, QN, PN]), ALU.mult)
                tt_(dve, x1, APi_t.re("p (j c) -> p j c", c=PN)[:, ka:kb].us(2).bc([128, 16, QN, PN]),
                    Z4[:, ka:kb, 0, 0:QN].us(3).bc([128, 16, QN, PN]), ALU.mult)
                t5 = TH[hh].re("p (j t q c) -> p j t q c", t=2, q=QN, c=PN)
                tt_(dve, zb5[:, ka:kb, 0], t5[:, :, 0], x0, ALU.add)
                tt_(dve, zb5[:, ka:kb, 1], t5[:, :, 1], x1, ALU.add)

        def ssm_down():
            for J in range(8):
                sb_ = sw()
                K.dma(gq, sb_.all(), WD_d[J])
                wc = sb_[:, 0:2048].re("p (r j t c) -> p r j t c", r=8, j=4, t=2)
                kd = sb_[:, 2048:3072].re("p (d c) -> p d c", d=8)
                bank = pf()
                ms = []
                for r in range(8):
                    cols = slice(64 * r, 64 * r + 64)
                    for r2 in range(r + 1):
                        ms.append(dict(out=bank[:, cols], a=kd[:, r - r2, :], b=Uv[:, J, r2::8],
                                       start=(r2 == 0), stop=False))
                    for j4 in range(4):
                        for part in range(2):
                            ms.append(dict(out=bank[32 * j4:32 * j4 + 32, cols], a=wc[:, r, j4, part, :],
                                           b=ZBv[:, 4 * J + j4, part, :], start=False,
                                           stop=(part == 1), tp=(0, 32 * j4)))
                K.mm(ms)
                actf(Gv[:, J, :].re("p (m r) -> p r m", r=8), bank.all().re("p (r m) -> p r m", r=8),
                     AF.Gelu_apprx_tanh)

        def stage4():
            for h4 in range(2):
                wb = ws()
                wv = wb.all().re("p (s k c) -> p s k c", s=2, k=8)
                load_w(f"glu{h4}", wb, [(wv[:, 0], w_glu_v[:, :, 512 * h4:512 * h4 + 512]),
                                        (wv[:, 1], w_glu_v[:, :, 1024 + 512 * h4:1024 + 512 * h4 + 512])])
                for cc in range(4):
                    co = 4 * h4 + cc
                    ba, bb = pf(), pf()
                    K.mm([dict(out=ba.all(), a=wv[:, 0, k, 128 * cc:128 * cc + 128], b=Gv[:, k, :],
                               start=(k == 0), stop=(k == 7)) for k in range(8)])
                    K.mm([dict(out=bb.all(), a=wv[:, 1, k, 128 * cc:128 * cc + 128], b=Gv[:, k, :],
                               start=(k == 0), stop=(k == 7)) for k in range(8)])
                    sg = SGS[co % 2]
                    actf(sg.all(), bb.all(), AF.Sigmoid)
                    tt_(dve, sg.all(), ba.all(), sg.all(), ALU.mult)
                    tt_(dve, YBv[:, co, :], sg.all(), SGv[:, co, :], ALU.mult)

        def stage5(row0):
            for t in range(TT):
                K.dma(sq, Hs[t], x_own[row0 + 128 * t:row0 + 128 * t + 128, :])
            for cg in range(4):
                wb = ws()
                wv = wb.all().re("p (k c) -> p k c", k=16)
                load_w(f"out{cg}", wb, [(wv, w_out_v[:, :, 512 * cg:512 * cg + 512])])
                for t in range(TT):
                    bank = pf()
                    ms = []
                    for k in range(16):
                        yv = YAv[:, k, 128 * t:128 * t + 128] if k < 8 else YBv[:, k - 8, 128 * t:128 * t + 128]
                        ms.append(dict(out=bank.all(), a=yv, b=wv[:, k, :], start=(k == 0), stop=(k == 15)))
                    K.mm(ms)
                    hv = Hs[t][:, 512 * cg:512 * cg + 512]
                    tt_(dve, hv, bank.all(), hv, ALU.add)

        def stage6(row0):
            for t in range(TT):
                XN = XNS[t % 2]
                actf(XN.all(), Hs[t], AF.Copy)
                for hk in range(2):
                    K.mm([dict(out=PT[hk][:, 128 * kk:128 * kk + 128],
                               a=XN[:, 128 * (8 * hk + kk):128 * (8 * hk + kk) + 128], b=IDB.all())
                          for kk in range(8)], transpose=True)
                    actf(HNTv[:, 8 * hk:8 * hk + 8, 128 * t:128 * t + 128],
                         PT[hk].all().re("p (k t) -> p k t", k=8), AF.Copy)
                pst = PST[t % 2]
                K.dma(gq, pst.all(), p_own[row0 + 128 * t:row0 + 128 * t + 128, :])
                K.mm([dict(out=PT[0][:, 128 * kk:128 * kk + 128], a=pst[:, 128 * kk:128 * kk + 128], b=IDB.all())
                      for kk in range(2)], transpose=True)
                actf(PTBv[:, :, 128 * t:128 * t + 128], PT[0][:, 0:256].re("p (k t) -> p k t", k=2), AF.Copy)

        def stage7():
            for cg in range(4):
                wb = ws()
                wv = wb.all().re("p (k c) -> p k c", k=16)
                load_w(f"pg{cg}", wb, [(wv, w_pg_v[:, :, 512 * cg:512 * cg + 512])])
                sb_ = sw()
                pv = sb_[:, 0:1024].re("p (k c) -> p k c", k=2)
                K.dma(gq, pv, w_ple_v[:, :, 512 * cg:512 * cg + 512])
                for t in range(TT):
                    bg, bp = pf(), pf()
                    K.mm([dict(out=bg.all(), a=HNTv[:, k, 128 * t:128 * t + 128], b=wv[:, k, :],
                               start=(k == 0), stop=(k == 15)) for k in range(16)])
                    K.mm([dict(out=bp.all(), a=PTBv[:, k, 128 * t:128 * t + 128], b=pv[:, k, :],
                               start=(k == 0), stop=(k == 1)) for k in range(2)])
                    sg = SGS[t % 2]
                    actf(sg.all(), bg.all(), AF.Sigmoid)
                    tt_(dve, sg.all(), bp.all(), sg.all(), ALU.mult)
                    hv = Hs[t][:, 512 * cg:512 * cg + 512]
                    tt_(dve, hv, hv, sg.all(), ALU.add)

        def stage8(row0):
            for t in range(TT):
                rms_stats(Hs[t], XNS[t % 2])
                stt_(dve, Hs[t], Hs[t], SS[:, 2:3], FG.all(), ALU.mult, ALU.mult)
                K.dma(sq, out_d[row0 + 128 * t:row0 + 128 * t + 128, :], Hs[t])

        def _lg(fn):
            def w(*a, **k):
                if _STAGELOG is not None:
                    _STAGELOG.append((fn.__name__, K.n_mm))
                return fn(*a, **k)
            return w
        stage0, stage1, stage2, ssm_up, ssm_scan, ssm_down = map(_lg, (stage0, stage1, stage2, ssm_up, ssm_scan, ssm_down))
        stage4, stage5, stage6, stage7, stage8 = map(_lg, (stage4, stage5, stage6, stage7, stage8))
        dve.memset(ZE[0].all(), 0.0)
        dve.memset(ZE[1].all(), 0.0)
        dve.memset(HALO.all(), 0.0)
        zi = 0
        npre = NPRE // NB
        nown = NOWN // NB
        UB = [SGv, Uv]

        def ub(i):
            return UB[(npre - 1 - i) % 2]

        def pre_chunks(i):
            return [4, 5] + ([0, 1] if i == npre - 1 else [])

        def seq0(i, tiles=None):
            if i < npre:
                stage0(x_pre, NB * i, tiles)
            elif i == npre:
                stage0(x_own, 0, tiles)

        seq0(0)
        stage1(pre_chunks(0), ub(0))
        seq0(1)
        for b in range(npre):
            if b + 1 < npre:
                stage1(pre_chunks(b + 1), ub(b + 1))
            else:
                cp_(dve, HALOv, PINall[:, :, 512:528])
                cp_(dve, PINall[:, :, 0:16], HALOv)
                stage1([4, 5])
            ssm_up((0,), ub(b))
            seq0(b + 2, (0, 1))
            ssm_up((1,), ub(b))
            seq0(b + 2, (2, 3))
            ssm_scan(ZE[zi], ZE[1 - zi], fixup=False)
            zi = 1 - zi
        for b in range(nown):
            row0 = NB * b
            if b > 0:
                cp_(dve, PINall[:, :, 0:16], HALOv)
                stage1([4, 5])
            ssm_up()
            ssm_scan(ZE[zi], ZE[1 - zi], fixup=True)
            zi = 1 - zi
            stage1([6, 7, 0, 1, 2, 3])
            stage2(first_own=(b == 0))
            ssm_down()
            stage4()
            stage5(row0)
            stage6(row0)
            stage7()
            if b + 1 < nown:
                stage0(x_own, NB * (b + 1))
            stage8(row0)
            flush_store()
        K.barrier()
    return nc


_NC = None


def kernel(x, p, norm_gain, w_in, w_pool, pool_scale, a_re, a_im, log_dt, b_re, b_im,
           c_re, c_im, d_skip, w_glu, w_out, w_ple, w_ple_gate, final_gain):
    global _NC
    f = lambda a: np.ascontiguousarray(np.asarray(a, dtype=np.float32))
    x = f(x)
    p = f(p)
    shared = {
        "norm_gain": f(norm_gain)[0], "w_in": f(w_in)[0], "w_pool": f(w_pool)[0],
        "pool_scale": f(pool_scale)[0], "a_re": f(a_re)[0], "a_im": f(a_im)[0], "log_dt": f(log_dt)[0],
        "b_re": f(b_re)[0], "b_im": f(b_im)[0], "c_re": f(c_re)[0], "c_im": f(c_im)[0],
        "d_skip": f(d_skip)[0], "w_glu": f(w_glu)[0], "w_out": f(w_out)[0], "w_ple": f(w_ple)[0],
        "w_ple_gate": f(w_ple_gate)[0], "final_gain": f(final_gain),
    }
    ident = np.eye(128, dtype=np.float32)
    mask = np.kron(np.eye(4, dtype=np.float32), np.ones((32, 32), np.float32))
    cf_first = np.ones((4, 16), np.float32)
    for g, w in enumerate(POOL_W):
        for t in range(16):
            cf_first[g, t] = w / min(t + 1, w)
    in_maps = []
    for c in range(8):
        b, h = divmod(c, 2)
        m = dict(shared)
        m["x_own"] = np.ascontiguousarray(x[b, h * NOWN:(h + 1) * NOWN])
        m["x_pre"] = np.ascontiguousarray(x[b, 0:NPRE]) if h == 1 else np.zeros((NPRE, D), np.float32)
        m["p_own"] = np.ascontiguousarray(p[0, b, h * NOWN:(h + 1) * NOWN])
        m["ident"] = ident
        m["mask"] = mask
        cfc = cf_first if h == 0 else np.ones((4, 16), np.float32)
        m["cf"] = np.ascontiguousarray(np.broadcast_to(cfc.reshape(1, 64), (128, 64)))
        in_maps.append(m)
    if _NC is None:
        _NC = build()
    res = run_bass_kernel_spmd(_NC, in_maps, core_ids=list(range(8)))
    out = np.empty((4, 4096, D), np.float32)
    for c in range(8):
        b, h = divmod(c, 2)
        out[b, h * NOWN:(h + 1) * NOWN] = res.results[c]["out"]
    return out
```

```python
import math
from contextlib import ExitStack

import numpy as np
import concourse.bass as bass
import concourse.mybir as mybir
from concourse.bass_utils import run_bass_kernel_spmd

F32 = mybir.dt.float32
BF16 = mybir.dt.bfloat16
I32 = mybir.dt.int32
AF = mybir.ActivationFunctionType
ALU = mybir.AluOpType

D = 2048
NOWN = 2048
NPRE = 2048
NB = 512
TT = NB // 128
MB = NB // 8
QN, PN = 8, 8
_DEBUG_STOP = None
_STAGELOG = None
POOL_W = (2, 4, 8, 16)
TWO_PI = float(2 * math.pi)
PI = float(math.pi)


class V:
    def __init__(self, buf, ap):
        self.buf, self.ap = buf, ap

    @property
    def bufs(self):
        return self.buf if isinstance(self.buf, tuple) else (self.buf,)

    def track(self, *bufs):
        return V(tuple(bufs), self.ap)

    def __getitem__(self, k):
        return V(self.buf, self.ap[k])

    def re(self, pat, **kw):
        return V(self.buf, self.ap.rearrange(pat, **kw))

    def bc(self, shape):
        return V(self.buf, self.ap.to_broadcast(list(shape)))

    def us(self, axis):
        return V(self.buf, self.ap.unsqueeze(axis))

    def cast(self, dt):
        return V(self.buf, self.ap.bitcast(dt))


class Buf:
    def __init__(self, base):
        self.base = base
        self.w = {}
        self.r = {}
        self.dsem = {}
        self.dcnt = {}

    def __getitem__(self, k):
        return V(self, self.base[k])

    def all(self):
        return V(self, self.base[:])


class Eng:
    def __init__(self, k, eng, sem):
        self.k, self.eng, self.sem = k, eng, sem
        self.cnt = 0
        self.known = {}

    def op(self, fn, **kw):
        reads, writes, call = [], [], {}
        for key, val in kw.items():
            if isinstance(val, V):
                (writes if key in ("out", "accum_out") else reads).extend(val.bufs)
                call[key] = val.ap
            else:
                call[key] = val
        self.k.sync(self, reads, writes)
        ins = getattr(self.eng, fn)(**call)
        self.k.done(self, ins, reads, writes)

    def memset(self, view, val):
        self.k.sync(self, [], list(view.bufs))
        ins = self.eng.memset(view.ap, val)
        self.k.done(self, ins, [], list(view.bufs))


class Kern:
    def __init__(self, nc, es):
        self.nc, self.es = nc, es
        self.nsem = 0
        self.n_mm = 0
        self.bufs = []
        self.pe = Eng(self, nc.tensor, self.sem("pe"))
        self.act = Eng(self, nc.scalar, self.sem("act"))
        self.dve = Eng(self, nc.vector, self.sem("dve"))
        self.sq = Eng(self, nc.sync, None)
        self.gq = Eng(self, nc.gpsimd, None)
        self.engs = [self.pe, self.act, self.dve, self.sq, self.gq]

    def sem(self, name):
        self.nsem += 1
        return self.es.enter_context(self.nc.semaphore(f"s{self.nsem}_{name}"))

    def sb(self, name, shape, dt):
        b = Buf(self.es.enter_context(self.nc.sbuf_tensor("b_" + name, list(shape), dt)))
        self.bufs.append(b)
        return b

    def ps(self, name, shape, dt):
        b = Buf(self.es.enter_context(self.nc.psum_tensor("q_" + name, list(shape), dt)))
        self.bufs.append(b)
        return b

    def dram(self, name, shape, dt):
        b = Buf(self.nc.dram_tensor(name, list(shape), dt).ap())
        self.bufs.append(b)
        return b

    def sync(self, E, reads, writes):
        deps = {}

        def upd(s, v):
            if deps.get(s, (None, 0))[1] < v:
                deps[s] = (s, v)

        for b in reads:
            for s, (so, v) in b.w.items():
                upd_key(deps, so, v)
        for b in writes:
            for s, (so, v) in b.w.items():
                if so is not E.sem or E is not self.pe:
                    upd_key(deps, so, v)
            for s, (so, v) in b.r.items():
                if so is not E.sem or E is not self.pe:
                    upd_key(deps, so, v)
        for key, (so, v) in deps.items():
            if E.known.get(key, 0) < v:
                E.eng.wait_ge(so, v)
                E.known[key] = v

    def done(self, E, ins, reads, writes, dma_buf=None):
        if dma_buf is not None:
            if E not in dma_buf.dsem:
                dma_buf.dsem[E] = self.sem("d")
                dma_buf.dcnt[E] = 0
            dma_buf.dcnt[E] += 16
            ins.then_inc(dma_buf.dsem[E], 16)
            so, v = dma_buf.dsem[E], dma_buf.dcnt[E]
        else:
            E.cnt += 1
            ins.then_inc(E.sem, 1)
            so, v = E.sem, E.cnt
        key = id(so)
        for b in reads:
            if b.r.get(key, (None, 0))[1] < v:
                b.r[key] = (so, v)
        for b in writes:
            b.r = {}
            b.w[key] = (so, v)

    def dma(self, Q, out, in_, **kw):
        reads = list(in_.bufs) if isinstance(in_, V) else []
        writes = list(out.bufs) if isinstance(out, V) else []
        self.sync(Q, reads, writes)
        oa = out.ap if isinstance(out, V) else out
        ia = in_.ap if isinstance(in_, V) else in_
        ins = Q.eng.dma_start(out=oa, in_=ia, **kw)
        db = (writes + reads)[0]
        self.done(Q, ins, reads, writes, dma_buf=db)

    def mm(self, mms, transpose=False):
        reads, writes = [], []
        for m in mms:
            writes.extend(m["out"].bufs)
            reads.extend(m["a"].bufs)
            reads.extend(m["b"].bufs)
        self.sync(self.pe, reads, writes)
        self.n_mm += len(mms)
        ins = None
        for m in mms:
            if transpose:
                ins = self.nc.tensor.transpose(m["out"].ap, m["a"].ap, m["b"].ap)
            else:
                kw = {}
                if m.get("tp") is not None:
                    kw["tile_position"] = m["tp"]
                ins = self.nc.tensor.matmul(m["out"].ap, lhsT=m["a"].ap, rhs=m["b"].ap,
                                            start=m.get("start", True), stop=m.get("stop", True), **kw)
        self.done(self.pe, ins, reads, writes)

    def barrier(self, skip=()):
        evs = {}
        for E in self.engs:
            if E.sem is not None and E.cnt > 0:
                evs[id(E.sem)] = (E.sem, E.cnt)
        for b in self.bufs:
            for Q, sm in b.dsem.items():
                evs[id(sm)] = (sm, b.dcnt[Q])
        for E in self.engs:
            if E in skip:
                continue
            for key, (so, v) in evs.items():
                if so is E.sem:
                    continue
                if E.known.get(key, 0) < v:
                    E.eng.wait_ge(so, v)
                    E.known[key] = v


def upd_key(deps, so, v):
    key = id(so)
    if deps.get(key, (None, 0))[1] < v:
        deps[key] = (so, v)


def build():
    nc = bass.Bass("TRN2", target_bir_lowering=False)

    def din(name, shape):
        return nc.dram_tensor(name, list(shape), F32, kind="ExternalInput").ap()

    x_own = din("x_own", [NOWN, D])
    x_pre = din("x_pre", [NPRE, D])
    p_own = din("p_own", [NOWN, 256])
    norm_gain = din("norm_gain", [D])
    w_in = din("w_in", [D, 4096])
    w_pool = din("w_pool", [4, 256, 256])
    pool_scale = din("pool_scale", [1024])
    a_re = din("a_re", [64, 64])
    a_im = din("a_im", [64, 64])
    log_dt = din("log_dt", [64])
    b_re = din("b_re", [64, 64, 16])
    b_im = din("b_im", [64, 64, 16])
    c_re = din("c_re", [64, 16, 64])
    c_im = din("c_im", [64, 16, 64])
    d_skip = din("d_skip", [1024])
    w_glu = din("w_glu", [1024, 2048])
    w_out = din("w_out", [D, D])
    w_ple = din("w_ple", [256, D])
    w_ple_gate = din("w_ple_gate", [D, D])
    final_gain = din("final_gain", [D])
    ident_d = din("ident", [128, 128])
    mask_d = din("mask", [128, 128])
    cf_d = din("cf", [128, 64])
    out_d = nc.dram_tensor("out", [NOWN, D], F32, kind="ExternalOutput").ap()

    with ExitStack() as es:
        K = Kern(nc, es)
        pe, act, dve, sq, gq = K.pe, K.act, K.dve, K.sq, K.gq

        WS = [K.sb(f"ws{i}", [128, 8192], BF16) for i in range(2)]
        SW = [K.sb(f"sw{i}", [128, 3072], BF16) for i in range(2)]
        XS = [K.sb(f"xs{i}", [128, 2048], F32) for i in range(2)]
        HNT = K.sb("hnt", [128, 8192], BF16)
        XNS = [K.sb(f"xn{i}", [128, 2048], BF16) for i in range(2)]
        XN = XNS[0]
        AR1 = K.sb("ar1", [128, 8768], F32)
        U = K.sb("u", [128, 4096], BF16)
        SG = K.sb("sg", [128, 4096], BF16)
        YA = K.sb("ya", [128, 4096], BF16)
        G = K.sb("g", [128, 4096], BF16)
        ZB = K.sb("zb", [128, 4096], BF16)
        PWT = K.sb("pwt", [128, 2 * 528], F32)
        HALO = K.sb("halo", [128, 8 * 16], BF16)
        ZE = [K.sb(f"ze{i}", [128, 64 * 9], F32) for i in range(2)]
        PTB = K.sb("ptb", [128, 2 * NB], BF16)
        PST = [K.sb(f"pst{i}", [128, 256], BF16) for i in range(2)]
        SGS = [K.sb(f"sgs{i}", [128, 512], F32) for i in range(2)]
        FG = K.sb("fg", [128, 2048], F32)
        WPL = K.sb("wpl", [128, 2 * 4 * 2 * 256], BF16)
        IDF = K.sb("idf", [128, 128], F32)
        IDB = K.sb("idb", [128, 128], BF16)
        MASK = K.sb("mask", [128, 128], F32)
        GT = K.sb("gt", [128, 16], F32)
        PSC = K.sb("psc", [128, 8], F32)
        DSK = K.sb("dsk", [128, 8], F32)
        CF = K.sb("cf", [128, 64], F32)
        SCN = K.sb("scn", [128, 1408], F32)
        SS = K.sb("ss", [128, 8], F32)
        EPS = K.sb("eps", [128, 2], F32)
        KI = K.sb("ki", [128, 32], I32)
        PF = [K.ps(f"pf{i}", [128, 512], F32) for i in range(6)]
        PT = [K.ps(f"pt{i}", [128, 1024], BF16) for i in range(2)]
        WU_d = K.dram("wu_d", [8, 128, 2048], BF16)
        WD_d = K.dram("wd_d", [8, 128, 3072], BF16)

        pfi = [0]

        def pf():
            b = PF[pfi[0] % 6]
            pfi[0] += 1
            return b

        def tt_(E, out, a, b, op):
            E.op("tensor_tensor", out=out, in0=a, in1=b, op=op)

        def ts_(E, out, a, s1, op0, s2=None, op1=None):
            if op1 is None:
                E.op("tensor_scalar", out=out, in0=a, scalar1=s1, scalar2=None, op0=op0)
            else:
                E.op("tensor_scalar", out=out, in0=a, scalar1=s1, scalar2=s2, op0=op0, op1=op1)

        def stt_(E, out, a, s, b, op0, op1):
            E.op("scalar_tensor_tensor", out=out, in0=a, scalar=s, in1=b, op0=op0, op1=op1)

        def cp_(E, out, a):
            E.op("tensor_copy", out=out, in_=a)

        def actf(out, a, func, **kw):
            act.op("activation", out=out, in_=a, func=func, **kw)

        K.dma(sq, IDF.all(), ident_d)
        K.dma(sq, MASK.all(), mask_d)
        K.dma(sq, CF.all(), cf_d)
        K.dma(sq, FG.all(), final_gain.partition_broadcast(128))
        K.dma(sq, AR1[0:16, 4096:4224], norm_gain.rearrange("(k p) -> k p", p=128))
        K.dma(sq, AR1[0:8, 4224:4352], pool_scale.rearrange("(k p) -> k p", p=128))
        K.dma(sq, AR1[0:8, 4352:4480], d_skip.rearrange("(k p) -> k p", p=128))
        K.dma(sq, AR1[0:32, 4480:4608], a_re.rearrange("(j t) n -> j (t n)", t=2))
        K.dma(sq, AR1[0:32, 4608:4736], a_im.rearrange("(j t) n -> j (t n)", t=2))
        K.dma(sq, AR1[0:32, 4864:4866], log_dt.rearrange("(j t) -> j t", t=2))
        K.dma(sq, AR1[0:32, 0:2048], b_re.rearrange("(j t) n c -> j (t n c)", t=2))
        K.dma(sq, AR1[0:32, 2048:4096], b_im.rearrange("(j t) n c -> j (t n c)", t=2))
        cp_(dve, AR1[0:32, 4736:4864].re("p (t n) -> p t n", t=2), AR1[0:32, 4864:4866].us(2).bc([32, 2, 64]))
        cp_(dve, IDB.all(), IDF.all())
        bank0 = pf()
        smalls = [(16, 4096, 0), (8, 4224, 16), (8, 4352, 24), (32, 4480, 32), (32, 4608, 64), (32, 4736, 96)]
        K.mm([dict(out=bank0[:, co:co + kk], a=AR1[0:kk, c0:c0 + 128], b=IDF[0:kk, 0:kk]) for kk, c0, co in smalls],
             transpose=True)
        cp_(dve, GT.all(), bank0[:, 0:16])
        cp_(dve, PSC.all(), bank0[:, 16:24])
        cp_(dve, DSK.all(), bank0[:, 24:32])
        cp_(dve, XS[0][:, 0:96], bank0[:, 32:128])
        dve.memset(EPS.all(), 1e-6)
        WPLv = WPL.all().re("p (v g c d) -> p v g c d", v=2, g=4, c=2)
        K.dma(gq, WPLv[:, 1], w_pool.rearrange("g (c p) d -> p g c d", p=128))
        for g in range(4):
            ts_(dve, WPLv[:, 0, g], WPLv[:, 1, g], 1.0 / POOL_W[g], ALU.mult)
        ts_(dve, WPLv[:, 1], WPLv[:, 1], -1.0, ALU.mult)

        if _DEBUG_STOP == 10:
            K.barrier()
            return nc
        SM = XS[0]

        def sl(i):
            return SM[:, 32 * i:32 * i + 32]

        are, aim, ldt = sl(0), sl(1), sl(2)
        lre, dt_, xr, mag, ang = sl(3), sl(4), sl(5), sl(6), sl(7)
        ts_(dve, lre, are, -1e-4, ALU.min)
        actf(dt_, ldt, AF.Exp)
        tt_(dve, xr, lre, dt_, ALU.mult)
        actf(mag, xr, AF.Exp)
        tt_(dve, ang, aim, dt_, ALU.mult)

        def range_reduce(dst, src, addc):
            t, kf, m = sl(8), sl(9), sl(10)
            ts_(dve, t, src, addc, ALU.add)
            ts_(dve, KI.all(), t, 1.0 / TWO_PI, ALU.mult)
            cp_(dve, kf, KI.all())
            stt_(dve, dst, kf, -TWO_PI, t, ALU.mult, ALU.add)
            ts_(dve, m, dst, PI, ALU.is_gt, TWO_PI, ALU.mult)
            tt_(dve, dst, dst, m, ALU.subtract)
            ts_(dve, m, dst, -PI, ALU.is_lt, TWO_PI, ALU.mult)
            tt_(dve, dst, dst, m, ALU.add)

        rs, rc, sn, cs = sl(11), sl(12), sl(13), sl(14)
        range_reduce(rs, ang, 0.0)
        range_reduce(rc, ang, PI / 2)
        actf(sn, rs, AF.Sin)
        actf(cs, rc, AF.Sin)
        abre, abim = sl(26), sl(27)
        tt_(dve, abre, mag, cs, ALU.mult)
        tt_(dve, abim, mag, sn, ALU.mult)
        den, rden, nre, t1, t2, qre, qim = sl(15), sl(16), sl(17), sl(18), sl(19), sl(20), sl(21)
        tt_(dve, den, lre, lre, ALU.mult)
        tt_(dve, t1, aim, aim, ALU.mult)
        tt_(dve, den, den, t1, ALU.add)
        dve.op("reciprocal", out=rden, in_=den)
        ts_(dve, nre, abre, -1.0, ALU.add)
        tt_(dve, t1, nre, lre, ALU.mult)
        tt_(dve, t2, abim, aim, ALU.mult)
        tt_(dve, t1, t1, t2, ALU.add)
        tt_(dve, qre, t1, rden, ALU.mult)
        tt_(dve, t1, abim, lre, ALU.mult)
        tt_(dve, t2, nre, aim, ALU.mult)
        tt_(dve, t1, t1, t2, ALU.subtract)
        tt_(dve, qim, t1, rden, ALU.mult)

        def cmul(ore, oim, xre, xim, yre, yim, tA, tB):
            tt_(dve, tA, xre, yre, ALU.mult)
            tt_(dve, tB, xim, yim, ALU.mult)
            tt_(dve, ore, tA, tB, ALU.subtract)
            tt_(dve, tA, xre, yim, ALU.mult)
            tt_(dve, tB, xim, yre, ALU.mult)
            tt_(dve, oim, tA, tB, ALU.add)

        def Pre(k):
            return sl(24 + 2 * k)

        def Pim(k):
            return sl(25 + 2 * k)

        dve.memset(Pre(0), 1.0)
        dve.memset(Pim(0), 0.0)
        for k in range(1, 8):
            cmul(Pre(k + 1), Pim(k + 1), Pre(k), Pim(k), abre, abim, sl(22), sl(23))
        Ar, Ai = Pre(8), Pim(8)

        def APre(p):
            return sl(44 + 2 * p)

        def APim(p):
            return sl(45 + 2 * p)

        dve.memset(APre(0), 1.0)
        dve.memset(APim(0), 0.0)
        for p in range(0, 8):
            cmul(APre(p + 1), APim(p + 1), APre(p), APim(p), Ar, Ai, sl(22), sl(23))
        Ar2 = SCN[:, 0:64]
        Ai_t = SCN[:, 64:96]
        nAi_t = SCN[:, 96:128]
        AQr2 = SCN[:, 128:192]
        AQi_t = SCN[:, 192:224]
        nAQi_t = SCN[:, 224:256]
        APr2 = SCN[:, 256:768]
        APi_t = SCN[:, 768:1024]
        nAPi_t = SCN[:, 1024:1280]
        AiS2 = SCN[:, 1280:1344]
        AQiS2 = SCN[:, 1344:1408]
        cp_(dve, Ar2.re("p (j t) -> p j t", t=2), Ar.us(2).bc([128, 32, 2]))
        ts_(dve, AiS2.re("p (j t) -> p j t", t=2)[:, :, 0], Ai, -1.0, ALU.mult)
        cp_(dve, AiS2.re("p (j t) -> p j t", t=2)[:, :, 1], Ai)
        ts_(dve, AQiS2.re("p (j t) -> p j t", t=2)[:, :, 0], APim(8), -1.0, ALU.mult)
        cp_(dve, AQiS2.re("p (j t) -> p j t", t=2)[:, :, 1], APim(8))
        cp_(dve, Ai_t, Ai)
        ts_(dve, nAi_t, Ai, -1.0, ALU.mult)
        cp_(dve, AQr2.re("p (j t) -> p j t", t=2), APre(8).us(2).bc([128, 32, 2]))
        cp_(dve, AQi_t, APim(8))
        ts_(dve, nAQi_t, APim(8), -1.0, ALU.mult)
        for p in range(8):
            cp_(dve, APr2.re("p (j t q) -> p j t q", t=2, q=8)[:, :, :, p], APre(p).us(2).bc([128, 32, 2]))
            cp_(dve, APi_t.re("p (j q) -> p j q", q=8)[:, :, p], APim(p))
            ts_(dve, nAPi_t.re("p (j q) -> p j q", q=8)[:, :, p], APim(p), -1.0, ALU.mult)

        if _DEBUG_STOP == 11:
            K.barrier()
            return nc
        XB = XS[1]

        def xb(i):
            return XB[:, 512 * i:512 * i + 512].re("p (j c) -> p j c", c=16)

        for part in range(2):
            bank = pf()
            K.mm([dict(out=bank[:, 32 * c:32 * c + 32],
                       a=AR1[0:32, 2048 * part:2048 * part + 2048].re("p (m c) -> p m c", c=16)[:, :, c],
                       b=IDF[0:32, 0:32]) for c in range(16)], transpose=True)
            cp_(dve, xb(part), bank.all().re("p (c j) -> p j c", c=16))
        TF = HNT.all().cast(F32)

        def tf(i):
            return TF[:, 512 * i:512 * i + 512].re("p (j c) -> p j c", c=16)

        def b16(v):
            return v.us(2).bc([128, 32, 16])

        cmul(xb(2), xb(3), xb(0), xb(1), b16(qre), b16(qim), tf(0), tf(1))

        if _DEBUG_STOP == 12:
            K.barrier()
            return nc
        CB = [AR1[0:32, 0:4096].re("p (j n) -> p j n", n=128), AR1[0:32, 4096:8192].re("p (j n) -> p j n", n=128)]
        dve.memset(AR1[0:32, 0:8192], 0.0)
        for g2 in range(2):
            for part, csrc in enumerate((c_re, c_im)):
                K.dma(sq, CB[part][16 * g2:16 * g2 + 16, :, 64 * g2:64 * g2 + 64],
                      csrc.rearrange("(j t) c n -> t c j n", t=2)[g2])
        SGF = SG.all().cast(F32)
        GF = G.all().cast(F32)
        CST = [SGF[:, 0:1024], SGF[:, 1024:2048]]
        NCSTI = GF[:, 0:1024]
        for part in range(2):
            for hb in range(2):
                bank = pf()
                K.mm([dict(out=bank[:, 32 * jj:32 * jj + 32], a=CB[part][:, 16 * hb + jj, :], b=IDF[0:32, 0:32])
                      for jj in range(16)], transpose=True)
                actf(CST[part][:, 512 * hb:512 * hb + 512], bank.all(), AF.Copy)
        ts_(dve, NCSTI, CST[1], -1.0, ALU.mult)

        def c3(v):
            return v.re("p (j c) -> p j c", c=32)

        if _DEBUG_STOP == 13:
            K.barrier()
            return nc
        WC_dv = WD_d[:, :, 0:2048].re("J p (r j4 tc) -> p J r j4 tc", r=8, j4=4)
        for r in range(8):
            pr = Pre(r + 1).us(2).bc([128, 32, 32])
            pi = Pim(r + 1).us(2).bc([128, 32, 32])
            ta, tb = c3(TF[:, 0:1024]), c3(TF[:, 1024:2048])
            WCraw = WS[r % 2][:, 4096:6144]
            WCst = WCraw.re("p (j t c) -> p j t c", t=2, c=32)
            tt_(dve, ta, c3(CST[0]), pr, ALU.mult)
            tt_(dve, tb, c3(CST[1]), pi, ALU.mult)
            tt_(dve, WCst[:, :, 0, :], ta, tb, ALU.subtract)
            tt_(dve, ta, c3(NCSTI), pr, ALU.mult)
            tt_(dve, tb, c3(CST[0]), pi, ALU.mult)
            tt_(dve, WCst[:, :, 1, :], ta, tb, ALU.subtract)
            K.dma(sq, WC_dv[:, :, r], WCraw.re("p (J j4 tc) -> p J j4 tc", J=8, j4=4))

        if _DEBUG_STOP == 14:
            K.barrier()
            return nc
        UF = U.all().cast(F32)
        SD = [UF[:, 0:1024].re("p (j c) -> p j c", c=32), UF[:, 1024:2048].re("p (j c) -> p j c", c=32)]
        dve.memset(UF, 0.0)
        WU_dv = WU_d.all().re("J p (r t c) -> p J r t c", r=8, t=2)
        KD_dv = WD_d[:, :, 2048:3072].re("J p (d c) -> p J d c", d=8)
        TK = TF[:, 2048:3072].re("p (J c) -> p J c", c=128)
        for d in range(8):
            r = 7 - d
            WUraw = WS[d % 2][:, 0:2048]
            WUst = WUraw.re("p (J t c) -> p J t c", t=2, c=128)
            KDst = WS[d % 2][:, 2048:3072].re("p (J c) -> p J c", c=128)
            cmul(tf(2), tf(3), xb(2), xb(3), b16(Pre(d)), b16(Pim(d)), tf(0), tf(1))
            for g2 in range(2):
                ps_ = slice(64 * g2, 64 * g2 + 64)
                cp_(dve, SD[0][ps_, :, 16 * g2:16 * g2 + 16], tf(2)[ps_])
                cp_(dve, SD[1][ps_, :, 16 * g2:16 * g2 + 16], tf(3)[ps_])
            for bq in range(4):
                bank = pf()
                ms = []
                for ii in range(4):
                    J, part = divmod(4 * bq + ii, 2)
                    ms.append(dict(out=bank[:, 128 * ii:128 * ii + 128],
                                   a=UF[:, 1024 * part + 128 * J:1024 * part + 128 * J + 128], b=IDF.all()))
                K.mm(ms, transpose=True)
                actf(WUraw[:, 512 * bq:512 * bq + 512], bank.all(), AF.Copy)
            K.dma(sq, WU_dv[:, :, r], WUst)
            for hb in range(2):
                bank = pf()
                ms = []
                for ii in range(4):
                    J = 4 * hb + ii
                    o = bank[:, 128 * ii:128 * ii + 128]
                    ms.append(dict(out=o, a=UF[:, 128 * J:128 * J + 128], b=CST[0][:, 128 * J:128 * J + 128],
                                   start=True, stop=False))
                    ms.append(dict(out=o, a=UF[:, 1024 + 128 * J:1024 + 128 * J + 128], b=NCSTI[:, 128 * J:128 * J + 128],
                                   start=False, stop=True))
                K.mm(ms)
                bv = bank.all().re("p (J c) -> p J c", c=128)
                mk = MASK.all().us(1).bc([128, 4, 128])
                if d == 0:
                    tt_(dve, TK[:, 4 * hb:4 * hb + 4, :], bv, mk, ALU.mult)
                    for ii in range(4):
                        J = 4 * hb + ii
                        stt_(dve, KDst[:, J, :], IDF.all(), DSK[:, J:J + 1], TK[:, J, :], ALU.mult, ALU.add)
                else:
                    tt_(dve, KDst[:, 4 * hb:4 * hb + 4, :], bv, mk, ALU.mult)
            K.dma(sq, KD_dv[:, :, d], KDst)

        K.barrier(skip=(gq,))
        if _DEBUG_STOP == 1:
            return nc

        HNTv = HNT.all().re("p (k t) -> p k t", k=16)
        R_PIN, R_PG, R_VX = Buf(None), Buf(None), Buf(None)
        K.bufs += [R_PIN, R_PG, R_VX]
        PINall = AR1[:, 0:2112].cast(BF16).re("p (i c) -> p i c", c=528).track(R_PIN)
        PGall = AR1[:, 2112:4160].cast(BF16).re("p (i c) -> p i c", c=512).track(R_PG)
        VX = AR1[:, 4160:8768].re("p (jt q c) -> p jt q c", q=QN, c=PN + 1).track(R_VX)
        VX5 = AR1[:, 4160:8768].re("p (j t q c) -> p j t q c", t=2, q=QN, c=PN + 1).track(R_VX)
        Hreg = [(R_PIN,), (R_PIN, R_PG), (R_PG, R_VX), (R_VX,)]
        Hs = [AR1[:, 2048 * t:2048 * t + 2048].track(*Hreg[t]) for t in range(TT)]
        Uv = U.all().re("p (i c) -> p i c", c=512)
        SGv = SG.all().re("p (i c) -> p i c", c=512)
        YAv = YA.all().re("p (i c) -> p i c", c=512)
        YBv = ZB.all().re("p (i c) -> p i c", c=512)
        Gv = G.all().re("p (i c) -> p i c", c=512)
        ZBv = ZB.all().re("p (j t m) -> p j t m", t=2, m=MB)
        HALOv = HALO.all().re("p (i c) -> p i c", c=16)
        PTBv = PTB.all().re("p (k t) -> p k t", k=2)
        TH = [YA.all().cast(F32), G.all().cast(F32)]
        T1 = TH[0]
        w_in_v = w_in.rearrange("(k p) c -> p k c", p=128)
        w_glu_v = w_glu.rearrange("(k p) c -> p k c", p=128)
        w_out_v = w_out.rearrange("(k p) c -> p k c", p=128)
        w_pg_v = w_ple_gate.rearrange("(k p) c -> p k c", p=128)
        w_ple_v = w_ple.rearrange("(k p) c -> p k c", p=128)
        wsi = [0]
        swi = [0]

        def ws():
            b = WS[wsi[0] % 2]
            wsi[0] += 1
            return b

        def sw():
            b = SW[swi[0] % 2]
            swi[0] += 1
            return b

        wcache = {}
        pend = []

        def flush_store():
            while pend:
                key, slot = pend.pop(0)
                K.dma(gq, wcache[key].all(), slot.all())

        def load_w(key, slot, loads):
            if key in wcache:
                flush_store()
                K.dma(gq, slot.all(), wcache[key].all())
                return
            for dst, src in loads:
                K.dma(gq, dst, src)
            flush_store()
            wcache[key] = K.dram("wsc_" + key, [128, 8192], BF16)
            pend.append((key, slot))

        def rms_stats(src, junk):
            actf(junk.all(), src, AF.Square, accum_out=SS[:, 0:1])
            actf(SS[:, 1:2], SS[:, 0:1], AF.Sqrt, bias=EPS[:, 0:1], scale=1.0 / D)
            dve.op("reciprocal", out=SS[:, 2:3], in_=SS[:, 1:2])

        def stage0(xsrc, row0, tiles=None):
            for t in (range(TT) if tiles is None else tiles):
                xs = XS[t % 2]
                K.dma(sq, xs.all(), xsrc[row0 + 128 * t:row0 + 128 * t + 128, :])
                XN = XNS[t % 2]
                rms_stats(xs.all(), XN)
                ts_(dve, XN.all(), xs.all(), SS[:, 2:3], ALU.mult)
                for hk in range(2):
                    K.mm([dict(out=PT[hk][:, 128 * kk:128 * kk + 128],
                               a=XN[:, 128 * (8 * hk + kk):128 * (8 * hk + kk) + 128], b=IDB.all())
                          for kk in range(8)], transpose=True)
                    tt_(dve, HNTv[:, 8 * hk:8 * hk + 8, 128 * t:128 * t + 128],
                        PT[hk].all().re("p (k t) -> p k t", k=8),
                        GT[:, 8 * hk:8 * hk + 8].us(2).bc([128, 8, 128]), ALU.mult)

        def stage1(chunks, uview=None):
            uview = Uv if uview is None else uview
            for ci in chunks:
                wb = ws()
                wv = wb.all().re("p (k c) -> p k c", k=16)
                load_w(f"win{ci}", wb, [(wv, w_in_v[:, :, 512 * ci:512 * ci + 512])])
                for cc in range(4):
                    c = 4 * ci + cc
                    grp, idx = divmod(c, 8)
                    bank = pf()
                    K.mm([dict(out=bank.all(), a=wv[:, k, 128 * cc:128 * cc + 128], b=HNTv[:, k, :],
                               start=(k == 0), stop=(k == 15)) for k in range(16)])
                    if grp == 0:
                        actf(PINall[:, idx, 16:528], bank.all(), AF.Copy)
                    elif grp == 1:
                        actf(PGall[:, idx, :], bank.all(), AF.Silu)
                    elif grp == 2:
                        actf(uview[:, idx, :], bank.all(), AF.Copy)
                    else:
                        actf(SGv[:, idx, :], bank.all(), AF.Silu)

        def stage2(first_own):
            PA = PWT[:, 0:528]
            PB = PWT[:, 528:1056]
            for i in range(8):
                g4 = i // 2
                w = POOL_W[g4]
                a1 = PINall[:, i, :]
                tt_(dve, PA[:, 1:528], a1[:, 1:528], a1[:, 0:527], ALU.add)
                cur = PA
                if w >= 4:
                    tt_(dve, PB[:, 3:528], PA[:, 3:528], PA[:, 1:526], ALU.add)
                    cur = PB
                if w >= 8:
                    tt_(dve, PA[:, 7:528], PB[:, 7:528], PB[:, 3:524], ALU.add)
                    cur = PA
                if w >= 16:
                    tt_(dve, PB[:, 15:528], PA[:, 15:528], PA[:, 7:520], ALU.add)
                    cur = PB
                if first_own:
                    tt_(dve, cur[:, 16:32], cur[:, 16:32], CF[:, 16 * g4:16 * g4 + 16], ALU.mult)
                cp_(dve, Gv[:, i, :], cur[:, 16:528])
            cp_(dve, HALOv, PINall[:, :, 512:528])
            for g4 in range(4):
                for co in range(2):
                    io = 2 * g4 + co
                    bank = pf()
                    ms = []
                    for c2 in range(2):
                        ms.append(dict(out=bank.all(), a=WPLv[:, 0, g4, c2, 128 * co:128 * co + 128],
                                       b=Gv[:, 2 * g4 + c2, :], start=(c2 == 0), stop=False))
                        ms.append(dict(out=bank.all(), a=WPLv[:, 1, g4, c2, 128 * co:128 * co + 128],
                                       b=PINall[:, 2 * g4 + c2, 16:528], start=False, stop=(c2 == 1)))
                    K.mm(ms)
                    stt_(dve, YAv[:, io, :], bank.all(), PSC[:, io:io + 1], PGall[:, io, :], ALU.mult, ALU.mult)

        def ssm_up(Jgs=(0, 1), uview=None):
            uview = Uv if uview is None else uview
            if 0 in Jgs:
                dve.memset(VX[:, :, :, 0:1], 0.0)
            for Jg in Jgs:
                banks = [pf() for _ in range(4)]
                for Jl in range(4):
                    J = 4 * Jg + Jl
                    sb_ = sw()
                    K.dma(gq, sb_[:, 0:2048], WU_d[J])
                    wv = sb_[:, 0:2048].re("p (r t c) -> p r t c", r=8, t=2)
                    ms = []
                    for part in range(2):
                        for r in range(8):
                            for j4 in range(4):
                                rows = slice(32 * j4, 32 * j4 + 32)
                                o = banks[j4][:, 128 * Jl + 64 * part:128 * Jl + 64 * part + 64]
                                ms.append(dict(out=o, a=wv[rows, r, part, :], b=uview[rows, J, r::8],
                                               start=(r == 0), stop=(r == 7), tp=(32 * j4, 0)))
                    K.mm(ms)
                for j4 in range(4):
                    for Jl in range(4):
                        J = 4 * Jg + Jl
                        actf(VX[:, 8 * J + 2 * j4:8 * J + 2 * j4 + 2, :, 1:PN + 1],
                             banks[j4][:, 128 * Jl:128 * Jl + 128].re("p (t q c) -> p t q c", t=2, q=QN), AF.Copy)

        def cstep(X64, Xsw, Ar2v, AiSv, ta, tb):
            tt_(dve, ta, X64, Ar2v, ALU.mult)
            tt_(dve, tb, Xsw, AiSv, ALU.mult)

        def ssm_scan(ze_in, ze_out, fixup):
            ta = T1[:, 0:512].re("p (a q) -> p a q", q=QN)
            tb = T1[:, 512:1024].re("p (j t q) -> p j t q", t=2, q=QN)
            tbf = T1[:, 512:1024].re("p (a q) -> p a q", q=QN)
            for c in range(2, PN + 1):
                cstep(VX[:, :, :, c - 1], VX5[:, :, ::-1, :, c - 1],
                      Ar2.us(2).bc([128, 64, QN]), AiS2.re("p (j t) -> p j t", t=2).us(3).bc([128, 32, 2, QN]), ta, tb)
                tt_(dve, VX[:, :, :, c], VX[:, :, :, c], ta, ALU.add)
                tt_(dve, VX[:, :, :, c], VX[:, :, :, c], tbf, ALU.add)
            cp_(dve, ze_out[:, 0:576].re("p (a q) -> p a q", q=9)[:, :, 0],
                ze_in[:, 0:576].re("p (a q) -> p a q", q=9)[:, :, QN])
            Z = ze_out.all().re("p (a q) -> p a q", q=9)
            Z4 = ze_out.all().re("p (j t q) -> p j t q", t=2, q=9)
            sa = T1[:, 1024:1088]
            sbv = T1[:, 1088:1152].re("p (j t) -> p j t", t=2)
            sbf = T1[:, 1088:1152]
            for q in range(QN):
                cstep(Z[:, :, q], Z4[:, :, ::-1, q], AQr2, AQiS2.re("p (j t) -> p j t", t=2), sa, sbv)
                tt_(dve, sa, sa, sbf, ALU.add)
                tt_(dve, Z[:, :, q + 1], sa, VX[:, :, q, PN], ALU.add)
            if not fixup:
                return
            zb5 = ZB.all().re("p (j t q c) -> p j t q c", t=2, q=QN, c=PN)
            for hh in range(2):
                ja, jb = 32 * hh, 32 * hh + 32
                ka, kb = 16 * hh, 16 * hh + 16
                tfh = TH[hh].re("p (a q c) -> p a q c", q=QN, c=PN)
                tt_(dve, tfh, APr2.re("p (a c) -> p a c", c=PN)[:, ja:jb].us(2).bc([128, 32, QN, PN]),
                    Z[:, ja:jb, 0:QN].us(3).bc([128, 32, QN, PN]), ALU.mult)
                tt_(dve, tfh, tfh, VX[:, ja:jb, :, 0:PN], ALU.add)
                x0 = XS[0][:, 1024 * hh:1024 * hh + 1024].re("p (j q c) -> p j q c", q=QN, c=PN)
                x1 = XS[1][:, 1024 * hh:1024 * hh + 1024].re("p (j q c) -> p j q c", q=QN, c=PN)
                tt_(dve, x0, nAPi_t.re("p (j c) -> p j c", c=PN)[:, ka:kb].us(2).bc([128, 16, QN, PN]),
                    Z4[:, ka:kb, 1, 0:QN].us(3).bc([128, 16, QN, PN]), ALU.mult)
                tt_(dve, x1, APi_t.re("p (j c) -> p j c", c=PN)[:, ka:kb].us(2).bc([128, 16, QN, PN]),
                    Z4[:, ka:kb, 0, 0:QN].us(3).bc([128, 16, QN, PN]), ALU.mult)
                t5 = TH[hh].re("p (j t q c) -> p j t q c", t=2, q=QN, c=PN)
                tt_(dve, zb5[:, ka:kb, 0], t5[:, :, 0], x0, ALU.add)
                tt_(dve, zb5[:, ka:kb, 1], t5[:, :, 1], x1, ALU.add)

        def ssm_down():
            flush_store()
            WDp = WD_d.all().re("J p c -> p J c")
            slotmap = {}
            plan = [((0, 1), WS[0]), ((2, 3), WS[1]), ((4,), None), ((5,), None), ((6, 7), WS[0])]
            for Js, big in plan:
                if big is None:
                    sb_ = sw()
                    K.dma(gq, sb_.all(), WD_d[Js[0]])
                    slotmap[Js[0]] = sb_[:, 0:3072]
                else:
                    K.dma(gq, big[:, 0:6144].re("p (J c) -> p J c", J=2), WDp[:, Js[0]:Js[0] + 2, :])
                    for i, J in enumerate(Js):
                        slotmap[J] = big[:, 3072 * i:3072 * i + 3072]
                for J in Js:
                    sv = slotmap[J]
                    wc = sv[:, 0:2048].re("p (r j t c) -> p r j t c", r=8, j=4, t=2)
                    kd = sv[:, 2048:3072].re("p (d c) -> p d c", d=8)
                    bank = pf()
                    ms = []
                    for r in range(8):
                        cols = slice(64 * r, 64 * r + 64)
                        for r2 in range(r + 1):
                            ms.append(dict(out=bank[:, cols], a=kd[:, r - r2, :], b=Uv[:, J, r2::8],
                                           start=(r2 == 0), stop=False))
                        for j4 in range(4):
                            for part in range(2):
                                ms.append(dict(out=bank[32 * j4:32 * j4 + 32, cols], a=wc[:, r, j4, part, :],
                                               b=ZBv[:, 4 * J + j4, part, :], start=False,
                                               stop=(part == 1), tp=(0, 32 * j4)))
                    K.mm(ms)
                    actf(Gv[:, J, :].re("p (m r) -> p r m", r=8), bank.all().re("p (r m) -> p r m", r=8),
                         AF.Gelu_apprx_tanh)

        def stage4():
            for h4 in range(2):
                wb = ws()
                wv = wb.all().re("p (s k c) -> p s k c", s=2, k=8)
                load_w(f"glu{h4}", wb, [(wv[:, 0], w_glu_v[:, :, 512 * h4:512 * h4 + 512]),
                                        (wv[:, 1], w_glu_v[:, :, 1024 + 512 * h4:1024 + 512 * h4 + 512])])
                for cc in range(4):
                    co = 4 * h4 + cc
                    ba, bb = pf(), pf()
                    K.mm([dict(out=ba.all(), a=wv[:, 0, k, 128 * cc:128 * cc + 128], b=Gv[:, k, :],
                               start=(k == 0), stop=(k == 7)) for k in range(8)])
                    K.mm([dict(out=bb.all(), a=wv[:, 1, k, 128 * cc:128 * cc + 128], b=Gv[:, k, :],
                               start=(k == 0), stop=(k == 7)) for k in range(8)])
                    sg = SGS[co % 2]
                    actf(sg.all(), bb.all(), AF.Sigmoid)
                    tt_(dve, sg.all(), ba.all(), sg.all(), ALU.mult)
                    tt_(dve, YBv[:, co, :], sg.all(), SGv[:, co, :], ALU.mult)

        def stage5(row0):
            for t in range(TT):
                K.dma(sq, Hs[t], x_own[row0 + 128 * t:row0 + 128 * t + 128, :])
            for cg in range(4):
                wb = ws()
                wv = wb.all().re("p (k c) -> p k c", k=16)
                load_w(f"out{cg}", wb, [(wv, w_out_v[:, :, 512 * cg:512 * cg + 512])])
                for t in range(TT):
                    bank = pf()
                    ms = []
                    for k in range(16):
                        yv = YAv[:, k, 128 * t:128 * t + 128] if k < 8 else YBv[:, k - 8, 128 * t:128 * t + 128]
                        ms.append(dict(out=bank.all(), a=yv, b=wv[:, k, :], start=(k == 0), stop=(k == 15)))
                    K.mm(ms)
                    hv = Hs[t][:, 512 * cg:512 * cg + 512]
                    tt_(dve, hv, bank.all(), hv, ALU.add)

        def stage6(row0):
            for t in range(TT):
                XN = XNS[t % 2]
                actf(XN.all(), Hs[t], AF.Copy)
                for hk in range(2):
                    K.mm([dict(out=PT[hk][:, 128 * kk:128 * kk + 128],
                               a=XN[:, 128 * (8 * hk + kk):128 * (8 * hk + kk) + 128], b=IDB.all())
                          for kk in range(8)], transpose=True)
                    actf(HNTv[:, 8 * hk:8 * hk + 8, 128 * t:128 * t + 128],
                         PT[hk].all().re("p (k t) -> p k t", k=8), AF.Copy)
                pst = PST[t % 2]
                K.dma(gq, pst.all(), p_own[row0 + 128 * t:row0 + 128 * t + 128, :])
                K.mm([dict(out=PT[0][:, 128 * kk:128 * kk + 128], a=pst[:, 128 * kk:128 * kk + 128], b=IDB.all())
                      for kk in range(2)], transpose=True)
                actf(PTBv[:, :, 128 * t:128 * t + 128], PT[0][:, 0:256].re("p (k t) -> p k t", k=2), AF.Copy)

        def stage7():
            for cg in range(4):
                wb = ws()
                wv = wb.all().re("p (k c) -> p k c", k=16)
                load_w(f"pg{cg}", wb, [(wv, w_pg_v[:, :, 512 * cg:512 * cg + 512])])
                sb_ = sw()
                pv = sb_[:, 0:1024].re("p (k c) -> p k c", k=2)
                K.dma(gq, pv, w_ple_v[:, :, 512 * cg:512 * cg + 512])
                for t in range(TT):
                    bg, bp = pf(), pf()
                    K.mm([dict(out=bg.all(), a=HNTv[:, k, 128 * t:128 * t + 128], b=wv[:, k, :],
                               start=(k == 0), stop=(k == 15)) for k in range(16)])
                    K.mm([dict(out=bp.all(), a=PTBv[:, k, 128 * t:128 * t + 128], b=pv[:, k, :],
                               start=(k == 0), stop=(k == 1)) for k in range(2)])
                    sg = SGS[t % 2]
                    actf(sg.all(), bg.all(), AF.Sigmoid)
                    tt_(dve, sg.all(), bp.all(), sg.all(), ALU.mult)
                    hv = Hs[t][:, 512 * cg:512 * cg + 512]
                    tt_(dve, hv, hv, sg.all(), ALU.add)

        def stage8(row0):
            for t in range(TT):
                rms_stats(Hs[t], XNS[t % 2])
                stt_(dve, Hs[t], Hs[t], SS[:, 2:3], FG.all(), ALU.mult, ALU.mult)
                K.dma(sq, out_d[row0 + 128 * t:row0 + 128 * t + 128, :], Hs[t])

        def _lg(fn):
            def w(*a, **k):
                if _STAGELOG is not None:
                    _STAGELOG.append((fn.__name__, K.n_mm))
                return fn(*a, **k)
            return w
        stage0, stage1, stage2, ssm_up, ssm_scan, ssm_down = map(_lg, (stage0, stage1, stage2, ssm_up, ssm_scan, ssm_down))
        stage4, stage5, stage6, stage7, stage8 = map(_lg, (stage4, stage5, stage6, stage7, stage8))
        dve.memset(ZE[0].all(), 0.0)
        dve.memset(ZE[1].all(), 0.0)
        dve.memset(HALO.all(), 0.0)
        zi = 0
        npre = NPRE // NB
        nown = NOWN // NB
        UB = [SGv, Uv]

        def ub(i):
            return UB[(npre - 1 - i) % 2]

        def pre_chunks(i):
            return [4, 5] + ([0, 1] if i == npre - 1 else [])

        def seq0(i, tiles=None):
            if i < npre:
                stage0(x_pre, NB * i, tiles)
            elif i == npre:
                stage0(x_own, 0, tiles)

        seq0(0)
        stage1(pre_chunks(0), ub(0))
        seq0(1)
        for b in range(npre):
            if b + 1 < npre:
                stage1(pre_chunks(b + 1), ub(b + 1))
            else:
                cp_(dve, HALOv, PINall[:, :, 512:528])
                cp_(dve, PINall[:, :, 0:16], HALOv)
                stage1([4, 5])
            ssm_up((0,), ub(b))
            seq0(b + 2, (0, 1))
            ssm_up((1,), ub(b))
            seq0(b + 2, (2, 3))
            ssm_scan(ZE[zi], ZE[1 - zi], fixup=False)
            zi = 1 - zi
        for b in range(nown):
            row0 = NB * b
            if b > 0:
                cp_(dve, PINall[:, :, 0:16], HALOv)
                stage1([4, 5])
            ssm_up()
            ssm_scan(ZE[zi], ZE[1 - zi], fixup=True)
            zi = 1 - zi
            stage1([6, 7, 0, 1, 2, 3])
            stage2(first_own=(b == 0))
            ssm_down()
            stage4()
            stage5(row0)
            stage6(row0)
            stage7()
            if b + 1 < nown:
                stage0(x_own, NB * (b + 1))
            stage8(row0)
            flush_store()
        K.barrier()
    return nc


_NC = None


def kernel(x, p, norm_gain, w_in, w_pool, pool_scale, a_re, a_im, log_dt, b_re, b_im,
           c_re, c_im, d_skip, w_glu, w_out, w_ple, w_ple_gate, final_gain):
    global _NC
    f = lambda a: np.ascontiguousarray(np.asarray(a, dtype=np.float32))
    x = f(x)
    p = f(p)
    shared = {
        "norm_gain": f(norm_gain)[0], "w_in": f(w_in)[0], "w_pool": f(w_pool)[0],
        "pool_scale": f(pool_scale)[0], "a_re": f(a_re)[0], "a_im": f(a_im)[0], "log_dt": f(log_dt)[0],
        "b_re": f(b_re)[0], "b_im": f(b_im)[0], "c_re": f(c_re)[0], "c_im": f(c_im)[0],
        "d_skip": f(d_skip)[0], "w_glu": f(w_glu)[0], "w_out": f(w_out)[0], "w_ple": f(w_ple)[0],
        "w_ple_gate": f(w_ple_gate)[0], "final_gain": f(final_gain),
    }
    ident = np.eye(128, dtype=np.float32)
    mask = np.kron(np.eye(4, dtype=np.float32), np.ones((32, 32), np.float32))
    cf_first = np.ones((4, 16), np.float32)
    for g, w in enumerate(POOL_W):
        for t in range(16):
            cf_first[g, t] = w / min(t + 1, w)
    in_maps = []
    for c in range(8):
        b, h = divmod(c, 2)
        m = dict(shared)
        m["x_own"] = np.ascontiguousarray(x[b, h * NOWN:(h + 1) * NOWN])
        m["x_pre"] = np.ascontiguousarray(x[b, 0:NPRE]) if h == 1 else np.zeros((NPRE, D), np.float32)
        m["p_own"] = np.ascontiguousarray(p[0, b, h * NOWN:(h + 1) * NOWN])
        m["ident"] = ident
        m["mask"] = mask
        cfc = cf_first if h == 0 else np.ones((4, 16), np.float32)
        m["cf"] = np.ascontiguousarray(np.broadcast_to(cfc.reshape(1, 64), (128, 64)))
        in_maps.append(m)
    if _NC is None:
        _NC = build()
    res = run_bass_kernel_spmd(_NC, in_maps, core_ids=list(range(8)))
    out = np.empty((4, 4096, D), np.float32)
    for c in range(8):
        b, h = divmod(c, 2)
        out[b, h * NOWN:(h + 1) * NOWN] = res.results[c]["out"]
    return out
```
